# Optimizing a Trainium2 kernel written in Bass

```python
import math
import jax, jax.numpy as jnp
from jax import lax
import numpy as np


D_MODEL = 1024
BATCH = 16
SEQ = 4096
DEPTH = 1

CHUNK = 64
Q_BLOCK = 128
MEM_LEN = 256
DA_HEADS = 4
DA_DH = 64
DA_WIDTH = DA_HEADS * 2 * DA_DH
RW_HEADS = 4
RW_DH = 64
RW_WIDTH = RW_HEADS * RW_DH
RW_DECAY_RANK = 64
RW_A_RANK = 64
RW_GATE_RANK = 128
RW_COLS = 3 * RW_WIDTH + RW_DECAY_RANK + RW_A_RANK + RW_GATE_RANK
XA_HEADS = 4
XA_DH = 64
XA_WIDTH = XA_HEADS * XA_DH
N_BRANCH = 3
PROJ_COLS = 3 * DA_WIDTH + RW_COLS + XA_WIDTH + N_BRANCH * D_MODEL
D_FF = 2816
ROPE_THETA = 500000.0
ROT_DIM = DA_DH // 4
EPS = 1e-6
RW_GN_EPS = 64e-5

kernel_name = 'hybrid_diffattn_rwkv7_memxattn_macaron'


def rms_norm(x, gain):
    xf = x.astype(jnp.float32)
    y = xf * lax.rsqrt(jnp.mean(xf * xf, axis=-1, keepdims=True) + EPS)
    return (y * gain.astype(jnp.float32)).astype(x.dtype)


def swiglu_half_step(x, norm_g, w1, w3, w2):
    h = rms_norm(x, norm_g)
    return x + 0.5 * ((jax.nn.silu(h @ w1) * (h @ w3)) @ w2)


def apply_partial_rope(t, positions):
    half = ROT_DIM // 2
    inv_freq = jnp.power(jnp.float32(ROPE_THETA), -jnp.arange(half, dtype=jnp.float32) * (2.0 / ROT_DIM))
    ang = positions.astype(jnp.float32)[..., None] * inv_freq
    cos = jnp.cos(ang)[:, :, None, :]
    sin = jnp.sin(ang)[:, :, None, :]
    tr = t[..., :ROT_DIM].astype(jnp.float32)
    t1, t2 = tr[..., :half], tr[..., half:]
    rot = jnp.concatenate([t1 * cos - t2 * sin, t2 * cos + t1 * sin], axis=-1).astype(t.dtype)
    return jnp.concatenate([rot, t[..., ROT_DIM:]], axis=-1)


def diff_attention(q, k, v, lam, positions):
    B, S, _ = q.shape
    q = apply_partial_rope(q.reshape(B, S, 2 * DA_HEADS, DA_DH), positions)
    k = apply_partial_rope(k.reshape(B, S, 2 * DA_HEADS, DA_DH), positions)
    q = q.reshape(B, S, DA_HEADS, 2, DA_DH).transpose(0, 2, 3, 1, 4)
    k = k.reshape(B, S, DA_HEADS, 2, DA_DH).transpose(0, 2, 3, 1, 4)
    v = v.reshape(B, S, DA_HEADS, 2 * DA_DH).transpose(0, 2, 1, 3)
    scale = DA_DH ** -0.5
    outs = []
    for i in range(S // Q_BLOCK):
        q0 = i * Q_BLOCK
        kv_len = q0 + Q_BLOCK
        s = jnp.einsum('bhmqd,bhmkd->bhmqk', q[:, :, :, q0:kv_len], k[:, :, :, :kv_len]).astype(jnp.float32) * scale
        q_chunk = (q0 + jnp.arange(Q_BLOCK)) // CHUNK
        k_chunk = jnp.arange(kv_len) // CHUNK
        allowed = k_chunk[None, :] <= q_chunk[:, None]
        p = jax.nn.softmax(jnp.where(allowed, s, -jnp.inf), axis=-1)
        attn = p[:, :, 0] - lam * p[:, :, 1]
        outs.append(jnp.einsum('bhqk,bhkd->bhqd', attn.astype(v.dtype), v[:, :, :kv_len]))
    return jnp.concatenate(outs, axis=2)


def token_shift(u, mu):
    prev = jnp.pad(u, ((0, 0), (1, 0), (0, 0)))[:, :-1]
    return u + (prev - u) * mu


def rwkv7_recurrence(r, w, k, v, a_vec, b_vec):
    B, S, H, N = r.shape

    def step(state, inp):
        r_t, w_t, k_t, v_t, a_t, b_t = inp
        sa = jnp.einsum('bhij,bhj->bhi', state, a_t)
        state = state * w_t[:, :, None, :] + sa[..., None] * b_t[:, :, None, :] + v_t[..., None] * k_t[:, :, None, :]
        return state, jnp.einsum('bhij,bhj->bhi', state, r_t)

    seq_first = lambda t: jnp.swapaxes(t, 0, 1)
    xs = (seq_first(r), seq_first(w), seq_first(k), seq_first(v), seq_first(a_vec), seq_first(b_vec))
    _, ys = lax.scan(step, jnp.zeros((B, H, N, N), jnp.float32), xs)
    return jnp.swapaxes(ys, 0, 1)


def rwkv7_time_mix(u, mu, w0, w2, a0, a2, g2, k_k, k_a, r_k, ln_w, ln_b):
    B, S, _ = u.shape
    u = token_shift(u, mu)
    i1 = RW_WIDTH
    i2 = 2 * RW_WIDTH
    i3 = 3 * RW_WIDTH
    i4 = i3 + RW_DECAY_RANK
    i5 = i4 + RW_A_RANK
    r, k, v, dw, da, dg = jnp.split(u, [i1, i2, i3, i4, i5], axis=-1)
    w_log = -jax.nn.softplus(-(w0 + jnp.tanh(dw) @ w2).astype(jnp.float32)) - 0.5
    decay = jnp.exp(-jnp.exp(w_log))
    a = jax.nn.sigmoid((a0 + da @ a2).astype(jnp.float32))
    g = (jax.nn.sigmoid(dg) @ g2).astype(jnp.float32)
    heads = lambda t: t.astype(jnp.float32).reshape(B, S, RW_HEADS, RW_DH)
    per_head = lambda p: p.astype(jnp.float32).reshape(RW_HEADS, RW_DH)
    r_h, k_h, v_h, a_h, w_h = heads(r), heads(k), heads(v), heads(a), heads(decay)
    kk = k_h * per_head(k_k)
    kk = kk / jnp.maximum(jnp.sqrt(jnp.sum(kk * kk, axis=-1, keepdims=True)), 1e-12)
    k_h = k_h * (1.0 + (a_h - 1.0) * per_head(k_a))
    y = rwkv7_recurrence(r_h, w_h, k_h, v_h, -kk, kk * a_h)
    mean = jnp.mean(y, axis=-1, keepdims=True)
    var = jnp.mean(jnp.square(y - mean), axis=-1, keepdims=True)
    y = (y - mean) * lax.rsqrt(var + RW_GN_EPS) * per_head(ln_w) + per_head(ln_b)
    y = y + jnp.sum(r_h * k_h * per_head(r_k), axis=-1, keepdims=True) * v_h
    return (y.reshape(B, S, RW_WIDTH) * g).astype(u.dtype)


def memory_attention(q, mem_h, w_kv):
    B, S, _ = q.shape
    M = mem_h.shape[1]
    k, v = jnp.split(mem_h @ w_kv, 2, axis=-1)
    q = q.reshape(B, S, XA_HEADS, XA_DH)
    k = k.reshape(B, M, XA_HEADS, XA_DH)
    v = v.reshape(B, M, XA_HEADS, XA_DH)
    s = jnp.einsum('bqhd,bkhd->bhqk', q, k).astype(jnp.float32) * (XA_DH ** -0.5)
    p = jax.nn.softmax(s, axis=-1)
    o = jnp.einsum('bhqk,bkhd->bqhd', p.astype(v.dtype), v)
    return o.reshape(B, S, XA_WIDTH)


def setup_inputs(seed: int = 0) -> dict:
    key = jax.random.key(seed)
    ks = iter(jax.random.split(key, 48))
    nrm = lambda shape, scale: jax.random.normal(next(ks), shape, jnp.float32) * scale
    gain = lambda shape: 1.0 + 0.02 * jax.random.normal(next(ks), shape, jnp.float32)
    L = DEPTH
    offsets = jax.random.randint(next(ks), (BATCH,), 0, 64) * CHUNK
    positions = (offsets[:, None] + jnp.arange(SEQ)[None, :]).astype(jnp.int32)
    return {
        'x': nrm((BATCH, SEQ, D_MODEL), 1.0),
        'mem': nrm((BATCH, MEM_LEN, D_MODEL), 1.0),
        'positions': positions,
        'ffn1_norm': gain((L, D_MODEL)),
        'ffn1_w1': nrm((L, D_MODEL, D_FF), D_MODEL ** -0.5),
        'ffn1_w3': nrm((L, D_MODEL, D_FF), D_MODEL ** -0.5),
        'ffn1_w2': nrm((L, D_FF, D_MODEL), D_FF ** -0.5),
        'mix_norm': gain((L, D_MODEL)),
        'w_in': nrm((L, D_MODEL, PROJ_COLS), D_MODEL ** -0.5),
        'gate_bias': nrm((L, N_BRANCH * D_MODEL), 0.02),
        'da_lambda_q1': nrm((L, DA_DH), 0.1),
        'da_lambda_k1': nrm((L, DA_DH), 0.1),
        'da_lambda_q2': nrm((L, DA_DH), 0.1),
        'da_lambda_k2': nrm((L, DA_DH), 0.1),
        'da_subln': gain((L, 2 * DA_DH)),
        'rw_mu': jax.random.uniform(next(ks), (L, RW_COLS), jnp.float32),
        'rw_w0': -1.0 - 5.0 * jax.random.uniform(next(ks), (L, RW_WIDTH), jnp.float32),
        'rw_w2': nrm((L, RW_DECAY_RANK, RW_WIDTH), 0.1 * RW_DECAY_RANK ** -0.5),
        'rw_a0': nrm((L, RW_WIDTH), 0.1),
        'rw_a2': nrm((L, RW_A_RANK, RW_WIDTH), 0.1 * RW_A_RANK ** -0.5),
        'rw_g2': nrm((L, RW_GATE_RANK, RW_WIDTH), RW_GATE_RANK ** -0.5),
        'rw_k_k': 0.85 + nrm((L, RW_WIDTH), 0.02),
        'rw_k_a': gain((L, RW_WIDTH)),
        'rw_r_k': nrm((L, RW_WIDTH), 0.1),
        'rw_ln_w': gain((L, RW_WIDTH)),
        'rw_ln_b': nrm((L, RW_WIDTH), 0.02),
        'mem_norm': gain((L, D_MODEL)),
        'w_mem_kv': nrm((L, D_MODEL, 2 * XA_WIDTH), D_MODEL ** -0.5),
        'w_up_a': nrm((L, DA_WIDTH, D_MODEL), DA_WIDTH ** -0.5),
        'w_up_b': nrm((L, RW_WIDTH, D_MODEL), RW_WIDTH ** -0.5),
        'w_up_c': nrm((L, XA_WIDTH, D_MODEL), XA_WIDTH ** -0.5),
        'w_out': nrm((L, D_MODEL, D_MODEL), D_MODEL ** -0.5),
        'ffn2_norm': gain((L, D_MODEL)),
        'ffn2_w1': nrm((L, D_MODEL, D_FF), D_MODEL ** -0.5),
        'ffn2_w3': nrm((L, D_MODEL, D_FF), D_MODEL ** -0.5),
        'ffn2_w2': nrm((L, D_FF, D_MODEL), D_FF ** -0.5),
        'final_norm': gain((D_MODEL,)),
    }


def reference(x, mem, positions, ffn1_norm, ffn1_w1, ffn1_w3, ffn1_w2, mix_norm, w_in, gate_bias,
              da_lambda_q1, da_lambda_k1, da_lambda_q2, da_lambda_k2, da_subln,
              rw_mu, rw_w0, rw_w2, rw_a0, rw_a2, rw_g2, rw_k_k, rw_k_a, rw_r_k, rw_ln_w, rw_ln_b,
              mem_norm, w_mem_kv, w_up_a, w_up_b, w_up_c, w_out,
              ffn2_norm, ffn2_w1, ffn2_w3, ffn2_w2, final_norm):
    B, S, D = x.shape
    c1 = DA_WIDTH
    c2 = 2 * DA_WIDTH
    c3 = 3 * DA_WIDTH
    c4 = c3 + RW_COLS
    c5 = c4 + XA_WIDTH
    for l in range(DEPTH):
        x = swiglu_half_step(x, ffn1_norm[l], ffn1_w1[l], ffn1_w3[l], ffn1_w2[l])
        h = rms_norm(x, mix_norm[l])
        proj = h @ w_in[l]
        q_da, k_da, v_da, u_rw, q_x, gate_lin = jnp.split(proj, [c1, c2, c3, c4, c5], axis=-1)
        lam_init = 0.8 - 0.6 * math.exp(-0.3 * l)
        lam = (jnp.exp(jnp.sum(da_lambda_q1[l] * da_lambda_k1[l]).astype(jnp.float32))
               - jnp.exp(jnp.sum(da_lambda_q2[l] * da_lambda_k2[l]).astype(jnp.float32)) + lam_init)
        o_a = diff_attention(q_da, k_da, v_da, lam, positions)
        o_a = rms_norm(o_a, da_subln[l]) * (1.0 - lam_init)
        o_a = o_a.transpose(0, 2, 1, 3).reshape(B, S, DA_WIDTH)
        o_b = rwkv7_time_mix(u_rw, rw_mu[l], rw_w0[l], rw_w2[l], rw_a0[l], rw_a2[l], rw_g2[l],
                             rw_k_k[l], rw_k_a[l], rw_r_k[l], rw_ln_w[l], rw_ln_b[l])
        o_c = memory_attention(q_x, rms_norm(mem, mem_norm[l]), w_mem_kv[l])
        gates = jax.nn.sigmoid((gate_lin + gate_bias[l]).astype(jnp.float32)).astype(x.dtype)
        gates = gates.reshape(B, S, N_BRANCH, D)
        merged = (gates[:, :, 0] * (o_a @ w_up_a[l])
                  + gates[:, :, 1] * (o_b @ w_up_b[l])
                  + gates[:, :, 2] * (o_c @ w_up_c[l]))
        x = x + merged @ w_out[l]
        x = swiglu_half_step(x, ffn2_norm[l], ffn2_w1[l], ffn2_w3[l], ffn2_w2[l])
    return rms_norm(x, final_norm)
```

```python
import math
import numpy as np
import concourse.bass as bass
import concourse.mybir as mybir
from concourse.bass_utils import run_bass_kernel_spmd

F32 = mybir.dt.float32
BF16 = mybir.dt.bfloat16
I32 = mybir.dt.int32
AF = mybir.ActivationFunctionType
ALU = mybir.AluOpType

D = 1024
DFF = 2816
TT = 512
CH = 64
MEM = 256
EPS = 1e-6
GN_EPS = 64e-5
LAM_INIT = 0.8 - 0.6 * math.exp(-0.3 * 0)
C0 = math.exp(-0.5)
C1_2PI = 6.28125
C2_2PI = 2 * math.pi - 6.28125
ESZ = {F32: 4, BF16: 2, I32: 4}
SLOT = 2048
RW_UNITS_CONST = 138
import os as _osf
FILL_A = int(_osf.environ.get('KFILL_A', 0))
FILL_B = int(_osf.environ.get('KFILL_B', 0))
NRING = 4


class V:
    def __init__(self, ap, space, esz, shape, strides, off, plo, phi):
        self.ap, self.space, self.esz = ap, space, esz
        self.shape, self.strides, self.off = list(shape), list(strides), off
        self.plo, self.phi = plo, phi

    def rng(self):
        if self.space.startswith("ps"):
            return (self.space, self.plo, self.phi, 0, 2048)
        hi = self.off + sum((n - 1) * s for n, s in zip(self.shape, self.strides)) + 1
        return (self.space, self.plo, self.phi, self.off * self.esz, hi * self.esz)

    def __getitem__(self, key):
        if not isinstance(key, tuple):
            key = (key,)
        pk = key[0]
        plo, phi = self.plo, self.phi
        if isinstance(pk, slice):
            a = 0 if pk.start is None else pk.start
            b = (self.phi - self.plo) if pk.stop is None else pk.stop
            plo, phi = self.plo + a, self.plo + b
        else:
            raise ValueError("partition index must be a slice")
        shape, strides, off = [], [], self.off
        fk = list(key[1:]) + [slice(None)] * (len(self.shape) - len(key) + 1)
        for k, n, s in zip(fk, self.shape, self.strides):
            if isinstance(k, slice):
                a = 0 if k.start is None else k.start
                b = n if k.stop is None else k.stop
                assert k.step is None and 0 <= a < b <= n, (k, n)
                off += a * s
                shape.append(b - a)
                strides.append(s)
            else:
                assert 0 <= k < n, (k, n)
                off += k * s
        return V(self.ap[key], self.space, self.esz, shape, strides, off, plo, phi)


class Op:
    __slots__ = ("eng", "idx", "fn", "deps", "ddeps", "signal", "dma", "ev", "dsem", "dval")

    def __init__(self, eng, idx, fn, dma):
        self.eng, self.idx, self.fn, self.dma = eng, idx, fn, dma
        self.deps = {}
        self.ddeps = []
        self.signal = False
        self.ev = None


ENGS = ("pe", "act", "dve", "pool", "sp")
NDSEM = {"sp": 6, "pool": 4, "act": 4}


class Prog:
    def __init__(self, nc):
        self.nc = nc
        self.ops = {e: [] for e in ENGS}
        self.spaces = {}
        self.ndma = {e: 0 for e in ENGS}
        self._bank = 0

    def _overl(self, r):
        sp, plo, phi, lo, hi = r
        lst = self.spaces.setdefault(sp, [])
        return lst, [e for e in lst if e[0] < phi and plo < e[1] and e[2] < hi and lo < e[3]]

    def _dep(self, op, prod, raw):
        if prod is None or prod is op:
            return
        if prod.dma:
            if prod not in op.ddeps:
                op.ddeps.append(prod)
            return
        if prod.eng == op.eng and not op.dma:
            if op.eng == "pe":
                return
        cur = op.deps.get(prod.eng)
        if cur is None or cur.idx < prod.idx:
            op.deps[prod.eng] = prod
        prod.signal = True

    def add(self, eng, fn, reads=(), writes=(), dma=False):
        op = Op(eng, len(self.ops[eng]), fn, dma)
        for v in reads:
            r = v.rng()
            lst, ov = self._overl(r)
            covered = False
            for e in ov:
                self._dep(op, e[4], True)
                e[5].append(op)
                if e[0] <= r[1] and e[1] >= r[2] and e[2] <= r[3] and e[3] >= r[4]:
                    covered = True
            if not covered:
                lst.append([r[1], r[2], r[3], r[4], None, [op]])
        for v in writes:
            r = v.rng()
            lst, ov = self._overl(r)
            for e in ov:
                self._dep(op, e[4], False)
                for rd in e[5]:
                    self._dep(op, rd, False)
                if r[1] <= e[0] and r[2] >= e[1] and r[3] <= e[2] and r[4] >= e[3]:
                    lst.remove(e)
            lst.append([r[1], r[2], r[3], r[4], op, []])
        self.ops[eng].append(op)
        return op

    def emit(self):
        nc = self.nc
        sems = []

        def newsem(name):
            s = nc.alloc_semaphore(name=name)
            sems.append(s)
            return s

        for eng in ("pe", "act", "dve", "pool"):
            cnt, sem, ep = 0, None, 0
            for op in self.ops[eng]:
                if op.dma or not op.signal:
                    continue
                if sem is None or cnt >= 30000:
                    sem = newsem(f"s_{eng}_{ep}")
                    ep += 1
                    cnt = 0
                cnt += 1
                op.ev = (sem, cnt)
        dsems = {}
        for eng in ENGS:
            k = 0
            for op in self.ops[eng]:
                if not op.dma:
                    continue
                n = NDSEM[eng]
                if (eng, k % n) not in dsems:
                    dsems[(eng, k % n)] = newsem(f"d_{eng}_{k % n}")
                op.dsem = dsems[(eng, k % n)]
                op.dval = 16 * (k // n + 1)
                k += 1

        def run(eng_name, e):
            seen = {}

            def wait(sem, val):
                if seen.get(id(sem), 0) >= val:
                    return
                e.wait_ge(sem, val)
                seen[id(sem)] = val

            last = {}
            for op in self.ops[eng_name]:
                for p in op.deps.values():
                    wait(*p.ev)
                for p in op.ddeps:
                    wait(p.dsem, p.dval)
                if op.dma:
                    if op.dval > 16:
                        wait(op.dsem, op.dval - 16)
                    op.fn(e).then_inc(op.dsem, 16)
                    last[id(op.dsem)] = (op.dsem, op.dval)
                else:
                    ins = op.fn(e)
                    if op.signal:
                        ins.then_inc(op.ev[0], 1)
            for sem, val in last.values():
                wait(sem, val)

        with nc.Block() as block:
            @block.tensor
            def _(e):
                run("pe", e)

            @block.scalar
            def _(e):
                run("act", e)

            @block.vector
            def _(e):
                run("dve", e)

            @block.gpsimd
            def _(e):
                run("pool", e)

            @block.sync
            def _(e):
                run("sp", e)


CST_COLS = {}
CST2_COLS = {}


def _cst_layout():
    off = 0
    for name, n in (("ident", 128), ("blk", 128), ("pm", 128), ("invf", 1), ("negpi", 1)):
        CST_COLS[name] = (off, n)
        off += n
    n1 = off
    off = 0
    for name, n in (("ones", 128), ("e0", 128), ("e1", 128), ("mlow", 512), ("mups", 512), ("mupi", 512),
                    ("rst", 512)):
        CST2_COLS[name] = (off, n)
        off += n
    return n1, off


NCST, NCST2 = _cst_layout()


def make_cst():
    c = np.zeros((128, NCST), np.float32)
    c2 = np.zeros((128, NCST2), np.float32)

    def put(name, arr):
        if name in CST_COLS:
            o, n = CST_COLS[name]
            c[:arr.shape[0], o:o + n] = arr
        else:
            o, n = CST2_COLS[name]
            c2[:arr.shape[0], o:o + n] = arr

    put("ident", np.eye(128, dtype=np.float32))
    blk = np.zeros((128, 128), np.float32)
    blk[:64, :64] = 1
    blk[64:, 64:] = 1
    put("blk", blk)
    pm = np.zeros((128, 128), np.float32)
    for g in range(2):
        for d in range(8):
            pm[g * 64 + d + 8, g * 64 + d] = -1.0
            pm[g * 64 + d, g * 64 + d + 8] = 1.0
    put("pm", pm)
    put("ones", np.ones((128, 128), np.float32))
    e0 = np.zeros((128, 128), np.float32)
    e0[:, :64] = 1
    e1 = np.zeros((128, 128), np.float32)
    e1[:, 64:] = 1
    put("e0", e0)
    put("e1", e1)
    i = np.arange(64)
    low = (i[None, :] < i[:, None]).astype(np.float32)
    ups = (i[:, None] < i[None, :]).astype(np.float32)
    upi = (i[:, None] <= i[None, :]).astype(np.float32)
    put("mlow", np.tile(low, (1, 8)))
    put("mups", np.tile(ups, (1, 8)))
    put("mupi", np.tile(upi, (1, 8)))
    rst = np.ones((128, 512), np.float32)
    rst[:, ::64] = 0
    put("rst", rst)
    invf = np.zeros((128, 1), np.float32)
    for g in range(2):
        for d in range(16):
            invf[g * 64 + d, 0] = np.float32(500000.0) ** np.float32(-(d % 8) * (2.0 / 16))
    put("invf", invf)
    put("negpi", np.full((128, 1), -math.pi, np.float32))
    return c, c2


CV = {}


def _cv_layout():
    off = 0
    for name, n in (("ffn1_norm", 8), ("mix_norm", 8), ("ffn2_norm", 8), ("final_norm", 8),
                    ("mem_norm", 8), ("gate_bias", 24), ("rw_mu", 8), ("rw_w0", 2), ("rw_a0", 2),
                    ("rw_k_k", 2), ("rw_k_a", 2), ("rw_r_k", 2), ("rw_ln_w", 2), ("rw_ln_b", 2),
                    ("da_subln", 1), ("lam", 4)):
        CV[name] = (off, n)
        off += n
    return off


NCV = _cv_layout()


def make_cvec(inp):
    c = np.zeros((128, NCV), np.float32)
    for name, (o, n) in CV.items():
        if name == "lam":
            for j, k in enumerate(("da_lambda_q1", "da_lambda_k1", "da_lambda_q2", "da_lambda_k2")):
                c[:64, o + j] = np.asarray(inp[k], np.float32).reshape(64)
        elif name == "da_subln":
            c[:, o] = np.asarray(inp[name], np.float32).reshape(128)
        else:
            c[:, o:o + n] = np.asarray(inp[name], np.float32).reshape(n, 128).T
    return c


def _colslots(w, c0, c1):
    out = []
    for c in range(c0, c1, 256):
        out.append(w[:, c:c + 256].reshape(8, 128, 256).transpose(1, 0, 2).reshape(128, 2048))
    return out


def _w2slots(w2):
    a = w2.reshape(22, 128, 8, 128).transpose(1, 2, 0, 3).reshape(128, 176, 128)
    return [a[:, s * 16:(s + 1) * 16, :].reshape(128, 2048) for s in range(11)]


def make_wslots(inp):
    g = lambda k: np.asarray(inp[k], np.float32)[0]
    slots = []
    idx = {}

    def reg(name, lst):
        idx[name] = (len(slots), len(lst))
        slots.extend(lst)

    for f in ("ffn1", "ffn2"):
        reg(f + "_w1", _colslots(g(f + "_w1"), 0, DFF))
        reg(f + "_w3", _colslots(g(f + "_w3"), 0, DFF))
        reg(f + "_w2", _w2slots(g(f + "_w2")))
    win = g("w_in")
    reg("w_qkvu", _colslots(win, 0, 2816))
    ua, ub, uc = g("w_up_a"), g("w_up_b"), g("w_up_c")
    mg = []
    for c in range(8):
        def gblk(br):
            col = 2816 + br * 1024 + c * 128
            return win[:, col:col + 128].reshape(8, 128, 128).transpose(1, 0, 2).reshape(128, 1024)
        sa = np.concatenate([gblk(0), gblk(1)], axis=1)
        ups = np.zeros((128, 1024), np.float32)
        o = 0
        for w_, nk in ((ua, 4), (ub, 2), (uc, 2)):
            for k in range(nk):
                ups[:, o:o + 128] = w_[k * 128:(k + 1) * 128, c * 128:(c + 1) * 128]
                o += 128
        sb_ = np.concatenate([gblk(2), ups], axis=1)
        mg.append(np.ascontiguousarray(sa))
        mg.append(np.ascontiguousarray(sb_))
    reg("merge", mg)
    reg("w_out", _colslots(g("w_out"), 0, 1024))
    reg("w_memkv", _colslots(g("w_mem_kv"), 0, 512))
    return np.ascontiguousarray(np.stack(slots)), idx


def make_smallw(inp):
    g = lambda k: np.asarray(inp[k], np.float32)[0]
    s = np.zeros((128, 512), np.float32)
    s[:64, :256] = g("rw_w2")
    s[64:, :256] = g("rw_a2")
    s[:, 256:512] = g("rw_g2")
    return s


_SLOT_IDX = None


class _Stop(Exception):
    pass


def build(NSEQ, S, slot_idx, nslots, taps=(), nt_limit=None, stop=None):
    import contextlib
    NT = S // TT
    NKB = S // 128
    nc = bass.Bass("TRN2", target_bir_lowering=False)
    P = Prog(nc)
    x_d = nc.dram_tensor("x", [NSEQ, S, D], F32, kind="ExternalInput").ap()
    mem_d = nc.dram_tensor("mem", [NSEQ, MEM, D], F32, kind="ExternalInput").ap()
    pos_d = nc.dram_tensor("pos", [NSEQ, S], I32, kind="ExternalInput").ap()
    cst_d = nc.dram_tensor("cst", [128, NCST], F32, kind="ExternalInput").ap()
    cst2_d = nc.dram_tensor("cst2", [128, NCST2], F32, kind="ExternalInput").ap()
    cv_d = nc.dram_tensor("cvec", [128, NCV], F32, kind="ExternalInput").ap()
    sw_d = nc.dram_tensor("smallw", [128, 512], F32, kind="ExternalInput").ap()
    wf_d = nc.dram_tensor("wslots", [nslots, 128, SLOT], F32, kind="ExternalInput").ap()
    wb_d = nc.dram_tensor("wbf", [nslots, 128, SLOT], BF16, kind="Internal").ap()
    out_d = nc.dram_tensor("out", [NSEQ, S, D], F32, kind="ExternalOutput").ap()
    stack = contextlib.ExitStack()

    def _strides(shape):
        st, s_ = [], 1
        for n in reversed(shape):
            st.insert(0, s_)
            s_ *= n
        return st

    def tile(name, shape, dt):
        t = stack.enter_context(nc.sbuf_tensor("sb_" + name, list(shape), dt))
        return V(t[:], name, ESZ[dt], shape[1:], _strides(shape[1:]), 0, 0, shape[0])

    def ptile(name):
        t = stack.enter_context(nc.psum_tensor(name, [128, 512], F32))
        return V(t[:], name, 4, [512], [1], 0, 0, 128)

    def dv(ap, name, idx=0):
        return V(ap, name, 1, [1], [1], idx, 0, 1)

    banks = [ptile(f"ps{i}") for i in range(8)]

    def bank(excl=()):
        for _ in range(9):
            P._bank = (P._bank + 1) % 8
            if P._bank not in excl:
                return banks[P._bank]

    PL = 46080
    ARENA = PL + 51200
    arena_t = stack.enter_context(nc.sbuf_tensor("arena", [128, ARENA // 4], F32))

    def carve(off, shape, dt):
        n = 1
        for k in shape[1:]:
            n *= k
        nb = n * ESZ[dt]
        assert off % 4 == 0 and nb % 4 == 0 and off + nb <= ARENA, (off, nb)
        ap = arena_t[:, off // 4:(off + nb) // 4]
        if dt != F32:
            ap = ap.bitcast(dt)
        if len(shape) == 3:
            ap = ap.rearrange("p (a b) -> p a b", a=shape[1])
        v = V(ap, "arena", ESZ[dt], shape[1:], _strides(shape[1:]), off // ESZ[dt], 0, 128)
        if shape[0] != 128:
            v = v[0:shape[0]]
        return v

    cst = tile("cst", [128, NCST], F32)
    NCB = 3 * 128 + 4 * 512
    cb = tile("cb", [128, NCB + 128], BF16)
    cv = tile("cv", [128, NCV], F32)
    smw = tile("smw", [128, 512], F32)
    cx = tile("cx", [128, 16], F32)
    blk64 = tile("blk64", [128, 128], F32)
    zt = tile("zt", [128, 128], F32)
    ring = [tile(f"ring{i}", [128, SLOT], BF16) for i in range(NRING)]
    kc = tile("kc", [128, 4, S], BF16)
    vc = tile("vc", [128, NKB, 512], BF16)
    kx = tile("kx", [128, 2, MEM], BF16)
    vx = tile("vx", [128, 8, 128], BF16)
    tbd = tile("tbd", [128, 2, 128], F32)
    xT = tile("xT", [128, 8, TT], F32)
    ucar = tile("ucar", [128, 8], F32)

    def cs(name):
        o, n = CST_COLS[name]
        return cst[:, o:o + n]

    def cvc(name, j=0):
        o, n = CV[name]
        return cv[:, o + j:o + j + 1]

    ident = cs("ident")
    epst = tile("epst", [128, 4], F32)
    EPSCOL = {EPS: 0, GN_EPS: 1, 0.0: 2}

    def eps_ap(e):
        return epst[:, EPSCOL[e]:EPSCOL[e] + 1]
    CB = {}
    for k_, (o_, n_) in CST2_COLS.items():
        CB[k_] = cb[:, o_:o_ + n_]
    CB["o1024"] = cb[:, NCB:NCB + 128]
    negpi = cs("negpi")

    last_rows = [None]

    def _rowguard(v, out=None, start=True):
        rows = (v.plo, v.phi)
        f32 = v.esz == 4
        prev = last_rows[0]
        if prev and rows != prev[:2] and rows != (0, 128) and prev[:2] != (0, 128):
            m_ = out.phi - out.plo
            o2 = out[:, 0:2]
            zl = zt[:, 0:m_]
            P.add("pe", lambda e: e.matmul(o2.ap, zl.ap, ident.ap[:, 0:2], start=start, stop=start),
                  reads=[zt] + ([] if start else [out]), writes=[out])
        last_rows[0] = rows + (f32,)

    def mm(out, lhsT, rhs, start=True, stop=True):
        _rowguard(lhsT, out, start)
        P.add("pe", lambda e: e.matmul(out.ap, lhsT.ap, rhs.ap, start=start, stop=stop),
              reads=[lhsT, rhs] + ([] if start else [out]), writes=[out])

    fill_rhs = cb[:, 0:512]

    def filler(n):
        for _ in range(n):
            last_rows[0] = (0, 128, False)
            dps = banks[3]
            P.add("pe", lambda e: e.matmul(dps.ap, CB["ones"].ap, fill_rhs.ap, start=True, stop=True), reads=[])

    def tr(out, in_):
        n = in_.phi - in_.plo
        idn = ident[0:n, 0:n]
        _rowguard(in_, out, True)
        P.add("pe", lambda e: e.transpose(out.ap, in_.ap, idn.ap), reads=[in_, idn], writes=[out])

    def act(out, in_, func, bias=None, scale=1.0):
        rd = [in_] + ([bias] if isinstance(bias, V) else [])
        b = bias.ap if isinstance(bias, V) else (0.0 if bias is None else bias)
        P.add("act", lambda e: e.activation(out.ap, in_.ap, func, bias=b, scale=scale), reads=rd, writes=[out])

    def tt(out, a, b, op, eng="dve"):
        P.add(eng, lambda e: e.tensor_tensor(out.ap, a.ap, b.ap, op), reads=[a, b], writes=[out])

    def ts(out, a, s1, op0, s2=None, op1=None, eng="dve"):
        rd = [a] + [s_ for s_ in (s1, s2) if isinstance(s_, V)]
        v1 = s1.ap if isinstance(s1, V) else s1
        v2 = s2.ap if isinstance(s2, V) else s2
        if op1 is None:
            P.add(eng, lambda e: e.tensor_scalar(out.ap, a.ap, v1, None, op0), reads=rd, writes=[out])
        else:
            P.add(eng, lambda e: e.tensor_scalar(out.ap, a.ap, v1, v2, op0, op1), reads=rd, writes=[out])

    def stt(out, a, s_, b, op0, op1):
        rd = [a, b] + ([s_] if isinstance(s_, V) else [])
        sv = s_.ap if isinstance(s_, V) else s_
        P.add("dve", lambda e: e.scalar_tensor_tensor(out.ap, a.ap, sv, b.ap, op0, op1), reads=rd, writes=[out])

    def cp(out, in_, eng="dve"):
        if eng == "act":
            P.add("act", lambda e: e.copy(out.ap, in_.ap), reads=[in_], writes=[out])
        else:
            P.add(eng, lambda e: e.tensor_copy(out.ap, in_.ap), reads=[in_], writes=[out])

    def recip(out, in_):
        P.add("dve", lambda e: e.reciprocal(out.ap, in_.ap), reads=[in_], writes=[out])

    def rsq(out, in_, eps):
        act(out, in_, AF.Sqrt, bias=eps_ap(eps))
        recip(out, out)

    def memset(out, val, eng="pool"):
        P.add(eng, lambda e: e.memset(out.ap, val), writes=[out])

    def dma(out, in_, q="sp"):
        P.add(q, lambda e: e.dma_start(out=out.ap, in_=in_.ap), reads=[in_], writes=[out], dma=True)

    evq = [0]

    def evac(out, in_):
        evq[0] ^= 1
        import os as _os2
        ev = _os2.environ.get("KEV")
        cp(out, in_, ev if ev else ("act" if int(in_.space[2:]) in (0, 2, 4) else "dve"))

    tap_done = set()
    tap_names = []

    def tap(name, v, shape, dt=F32):
        if name not in taps or name in tap_done:
            return
        tap_done.add(name)
        td = nc.dram_tensor("tap_" + name, list(shape), dt, kind="ExternalOutput").ap()
        tap_names.append("tap_" + name)
        dma(dv(td, "tap_" + name), v, "pool")

    wcnt = [0]

    def wslot(sid_):
        r = ring[wcnt[0] % NRING]
        wcnt[0] += 1
        dma(r, dv(wb_d[sid_], "wbf", sid_), "sp")
        return r

    def chk(stage):
        if stop == stage:
            raise _Stop()

    def sid(name, j=0):
        assert j < slot_idx[name][1]
        return slot_idx[name][0] + j

    sqb = carve(0, [128, TT], BF16)
    rstd = carve(1024, [128, TT], F32)
    silu_t = [carve(3072 + i * 2048, [128, TT], F32) for i in range(2)]
    gT = carve(7168, [128, 22, TT], BF16)
    mT = gT
    qT = carve(15360, [128, 4, TT], BF16)
    qxT = carve(19456, [128, 2, TT], BF16)
    oaT = carve(21504, [128, 4, TT], BF16)
    ocT = carve(25600, [128, 2, TT], BF16)
    obT = carve(27648, [128, 2, TT], BF16)
    us = carve(29696, [128, 8, TT], F32)
    hT = carve(PL, [128, 8, TT], BF16)
    xin = carve(PL + 8192, [128, D], F32)
    yout = carve(PL + 12288, [128, D], F32)
    o = PL + 16384
    posi = carve(o, [128, TT], I32); o += 2048
    ang = carve(o, [128, TT], F32); o += 2048
    cosT = carve(o, [128, TT], F32); o += 2048
    sinT = carve(o, [128, TT], F32); o += 2048
    tf = carve(o, [128, TT], F32); o += 2048
    ra = carve(o, [128, TT], F32); o += 2048
    rb = carve(o, [128, TT], F32); o += 2048
    u1 = carve(o, [128, TT + 1], F32); o += 2052
    o = PL + 34816
    tf_b = carve(o, [128, TT], F32); o += 2048
    ra_b = carve(o, [128, TT], F32); o += 2048
    rb_b = carve(o, [128, TT], F32); o += 2048
    u1_b = carve(o, [128, TT + 1], F32); o += 2052
    o = PL + 8192
    gb1 = carve(o + 3072, [128, TT], F32)
    gb2 = carve(o + 5120, [128, TT], F32)
    gb3 = carve(o + 7168, [128, TT], F32)
    o = 7168
    pT = [carve(o + i * 1024, [128, TT], BF16) for i in range(3)]; o += 3072
    at1 = carve(o, [128, TT], F32); o += 2048
    at2 = carve(o, [128, TT], F32); o += 2048
    at3 = silu_t[0]
    o = PL
    RW = {}
    for nm in ("sg", "csg", "E1", "E2", "E3", "asig", "kkn", "kmod", "thw", "tmp2", "vT"):
        RW[nm] = carve(o, [128, TT], F32); o += 2048
    for nm in ("L", "Q", "Mt", "L2", "Q2", "AKt", "RBt", "RKt", "Atok", "Btok", "Ktok"):
        RW[nm] = carve(o, [64, 512], F32); o += 2048
    RW["V0"] = carve(o, [64, 4, 128], F32); o += 2048
    RW["V1"] = carve(o, [64, 4, 128], F32); o += 2048
    RW["MAt"] = carve(o, [128, 4, 64], F32); o += 1024
    RW["U0"] = carve(o, [64, 128], F32); o += 512
    RW["U1"] = carve(o, [64, 128], F32); o += 512
    assert o <= ARENA, o

    def _body():
        dma(cst, dv(cst_d, "cst_d"))
        memset(epst[:, 0:1], EPS)
        memset(epst[:, 1:2], GN_EPS)
        memset(epst[:, 2:3], 0.0)
        memset(zt, 0.0)
        dma(cv, dv(cv_d, "cv_d"))
        dma(smw, dv(sw_d, "sw_d"))
        c2s = carve(PL + 24576, [128, NCST2], F32)
        dma(c2s, dv(cst2_d, "cst2_d"))
        cp(cb[:, 0:NCB], c2s)
        o_ = CST2_COLS["ones"][0]
        ts(CB["o1024"], c2s[:, o_:o_ + 128], 1.0 / 1024, ALU.mult)
        o128 = tile("o128", [128, 128], BF16)
        ts(o128, c2s[:, o_:o_ + 128], 1.0 / 128, ALU.mult)
        ka = CV["rw_k_a"][0]
        ts(cx[:, 0:2], cv[:, ka:ka + 2], -1.0, ALU.mult, 1.0, ALU.add)
        ts(cx[:, 2:3], cvc("da_subln"), 1.0 - LAM_INIT, ALU.mult)
        lo = CV["lam"][0]
        tt(cx[0:64, 4:5], cv[0:64, lo:lo + 1], cv[0:64, lo + 1:lo + 2], ALU.mult)
        tt(cx[0:64, 5:6], cv[0:64, lo + 2:lo + 3], cv[0:64, lo + 3:lo + 4], ALU.mult)
        b0 = bank()
        mm(b0[:, 0:2], c2s[0:64, o_:o_ + 128], cx[0:64, 4:6])
        act(cx[:, 6:8], b0[:, 0:2], AF.Exp)
        tt(cx[:, 8:9], cx[:, 7:8], cx[:, 6:7], ALU.subtract)
        ts(cx[:, 3:4], cx[:, 8:9], -LAM_INIT, ALU.add)
        neglam = cx[:, 3:4]
        subg = cx[:, 2:3]
        ts(blk64, cs("blk"), 1.0 / 64, ALU.mult)

        chk("pro0")
        stg_f = [carve(PL + i * 8192, [128, SLOT], F32) for i in range(3)]
        stg_b = [carve(PL + 36864 + i * 4096, [128, SLOT], BF16) for i in range(3)]
        import os as _os
        for s_ in range(int(_os.environ.get('KCAST0', 0)), int(_os.environ.get('KCAST', nslots))):
            f = stg_f[s_ % 3]
            b = stg_b[s_ % 3]
            dma(f, dv(wf_d[s_], "wf_d", s_), "sp")
            cp(b, f, "dve" if s_ % 2 == 0 else "act")
            dma(dv(wb_d[s_], "wbf", s_), b, "pool" if s_ % 2 == 0 else "act")
        chk("pro")
        sqs = [sqb, carve(5120, [128, TT], BF16), carve(6144, [128, TT], BF16)]

        def rmsnorm(src, gname, dst, n=TT):
            ps = bank()
            for c in range(8):
                sq_ = sqs[c % 3]
                act(sq_[:, 0:n], src[:, c, 0:n], AF.Square)
                mm(ps[:, 0:n], CB["o1024"], sq_[:, 0:n], start=(c == 0), stop=(c == 7))
            rsq(rstd[:, 0:n], ps[:, 0:n], EPS)
            for c in range(8):
                stt(dst[:, c, 0:n], src[:, c, 0:n], cvc(gname, c), rstd[:, 0:n], ALU.mult, ALU.mult)

        def ffn(pref):
            rmsnorm(xT, pref + "_norm", hT)
            for j in range(11):
                w1 = wslot(sid(pref + "_w1", j))
                w3 = wslot(sid(pref + "_w3", j))
                for jj in range(2):
                    pa, pb = bank(), bank()
                    for k in range(8):
                        mm(pa, w1[:, k * 256 + jj * 128:k * 256 + jj * 128 + 128], hT[:, k, :], k == 0, k == 7)
                    for k in range(8):
                        mm(pb, w3[:, k * 256 + jj * 128:k * 256 + jj * 128 + 128], hT[:, k, :], k == 0, k == 7)
                    st = silu_t[(2 * j + jj) % 2]
                    act(st, pa, AF.Silu)
                    tt(gT[:, 2 * j + jj, :], st, pb, ALU.mult)
            pair = 0
            w2 = None
            for c in range(8):
                ps = bank()
                for k in range(22):
                    if pair % 16 == 0:
                        w2 = wslot(sid(pref + "_w2", pair // 16))
                    o2 = (pair % 16) * 128
                    mm(ps, w2[:, o2:o2 + 128], gT[:, k, :], k == 0, k == 21)
                    pair += 1
                stt(xT[:, c, :], ps, 0.5, xT[:, c, :], ALU.mult, ALU.add)

        ldq = [0]
        xin_a = xin

        def load_T(src_ap, name, col0):
            ldq[0] ^= 1
            xin = xin_a if ldq[0] else yout
            dma(xin, dv(src_ap, name))
            chk("l0")
            for c4 in range(2):
                ps = bank()
                for c in range(4):
                    tr(ps[:, c * 128:(c + 1) * 128], xin[:, (c4 * 4 + c) * 128:(c4 * 4 + c + 1) * 128])
                    chk("l1")
                chk("l2")
                for c in range(4):
                    evac(xT[:, c4 * 4 + c, col0:col0 + 128], ps[:, c * 128:(c + 1) * 128])
                    chk("l3")
                    if c == 1:
                        chk("l3b")
                    if c == 2:
                        chk("l3c")
                chk("l4")

        def bfv(v, half):
            off_b = v.off * 4 + half * 1024
            return carve(off_b, [v.phi - v.plo, TT], BF16)

        ctx = dict(bfv=bfv, us=us, RW=RW, cs=cs, cvc=cvc, cx=cx, smw=smw, tbd=tbd, obT=obT, bank=bank, banks=banks, mm=mm,
                   tr=tr, act=act, tt=tt, ts=ts, stt=stt, cp=cp, evac=evac, memset=memset, blk64=blk64, CB=CB,
                   P=P, tap=tap, rsq=rsq, chk=chk)

        for s in range(NSEQ):
            memset(tbd, 0.0)
            memset(ucar, 0.0)
            memset(vx, 0.0)
            chk("m0")
            for b in range(2):
                load_T(mem_d[s, b * 128:(b + 1) * 128, :], "mem_d", b * 128)
            chk("m1")
            rmsnorm(xT, "mem_norm", hT, n=256)
            chk("m2")
            wk = wslot(sid("w_memkv", 0))
            for hc in range(2):
                ps = bank()
                for k in range(8):
                    mm(ps[:, 0:256], wk[:, k * 256 + hc * 128:k * 256 + hc * 128 + 128], hT[:, k, 0:256], k == 0, k == 7)
                evac(kx[:, hc, :], ps[:, 0:256])
            chk("m3")
            wv = wslot(sid("w_memkv", 1))
            for mb in range(2):
                ps = bank()
                for k in range(8):
                    mm(ps[:, 0:256], hT[:, k, mb * 128:(mb + 1) * 128], wv[:, k * 256:(k + 1) * 256], k == 0, k == 7)
                for h in range(4):
                    e_ = h % 2
                    evac(vx[:, mb * 4 + (h // 2) * 2 + e_, e_ * 64:e_ * 64 + 64], ps[:, h * 64:(h + 1) * 64])

            chk("mem")
            for t in range(NT if nt_limit is None else nt_limit):
                t0 = t * TT
                first = (s == 0 and t == 0)
                for b in range(4):
                    load_T(x_d[s, t0 + b * 128:t0 + (b + 1) * 128, :], "x_d", b * 128)
                chk("A")
                ffn("ffn1")
                if first:
                    tap("x1", xT, [128, 8, TT])
                chk("B")
                rmsnorm(xT, "mix_norm", hT)
                dma(posi, dv(pos_d[s, t0:t0 + TT].partition_broadcast(128), "pos_d"))
                cp(ang, posi)
                ts(ang, ang, cs("invf"), ALU.mult)
                for dst_, shift in ((sinT, 0.0), (cosT, 0.5 * math.pi)):
                    if shift:
                        ts(rb, ang, shift, ALU.add)
                        a_ = rb
                    else:
                        a_ = ang
                    ts(ra, a_, 1.0 / (2 * math.pi), ALU.mult)
                    cp(posi, ra)
                    cp(ra, posi)
                    stt(tf, ra, -C1_2PI, a_, ALU.mult, ALU.add)
                    stt(tf, ra, -C2_2PI, tf, ALU.mult, ALU.add)
                    ts(ra, tf, math.pi, ALU.is_gt)
                    stt(tf, ra, -2 * math.pi, tf, ALU.mult, ALU.add)
                    ts(ra, tf, -math.pi, ALU.is_lt)
                    stt(tf, ra, 2 * math.pi, tf, ALU.mult, ALU.add)
                    act(dst_, tf, AF.Sin)

                rq = [0]

                def rope_chunk(w, jj, dst):
                    rq[0] ^= 1
                    tf_, ra_, rb_ = (tf, ra, rb) if rq[0] else (tf_b, ra_b, rb_b)
                    ps = bank()
                    for k in range(8):
                        mm(ps, w[:, k * 256 + jj * 128:k * 256 + jj * 128 + 128], hT[:, k, :], k == 0, k == 7)
                    cp(tf_, ps, "act")
                    pr = bank()
                    mm(pr, cs("pm"), tf_)
                    tt(ra_, tf_, cosT, ALU.mult)
                    tt(rb_, pr, sinT, ALU.mult)
                    tt(dst, ra_, rb_, ALU.add)

                wq = [wslot(sid("w_qkvu", j)) for j in range(2)]
                for c in range(4):
                    rope_chunk(wq[c // 2], c % 2, qT[:, c, :])
                wkk = [wslot(sid("w_qkvu", 2 + j)) for j in range(2)]
                for c in range(4):
                    rope_chunk(wkk[c // 2], c % 2, kc[:, c, t0:t0 + TT])
                wvv = [wslot(sid("w_qkvu", 4 + j)) for j in range(2)]
                for b in range(4):
                    ps = bank()
                    for j in range(2):
                        for k in range(8):
                            mm(ps[:, j * 256:(j + 1) * 256], hT[:, k, b * 128:(b + 1) * 128],
                               wvv[j][:, k * 256:(k + 1) * 256], k == 0, k == 7)
                    evac(vc[:, t * 4 + b, :], ps)
                for j in range(4):
                    wu = wslot(sid("w_qkvu", 6 + j))
                    for jj in range(2):
                        c = 2 * j + jj
                        ps = bank()
                        for k in range(8):
                            mm(ps, wu[:, k * 256 + jj * 128:k * 256 + jj * 128 + 128], hT[:, k, :], k == 0, k == 7)
                        u1_, tf_ = (u1, tf) if c % 2 == 0 else (u1_b, tf_b)
                        cp(u1_[:, 0:1], ucar[:, c:c + 1], "dve")
                        cp(u1_[:, 1:TT + 1], ps, "act")
                        cp(ucar[:, c:c + 1], u1_[:, TT:TT + 1], "dve")
                        tt(tf_, u1_[:, 0:TT], u1_[:, 1:TT + 1], ALU.subtract)
                        stt(us[:, c, :], tf_, cvc("rw_mu", c), u1_[:, 1:TT + 1], ALU.mult, ALU.add)
                wqx = wslot(sid("w_qkvu", 10))
                for c in range(2):
                    ps = bank()
                    for k in range(8):
                        mm(ps, wqx[:, k * 256 + c * 128:k * 256 + c * 128 + 128], hT[:, k, :], k == 0, k == 7)
                    evac(qxT[:, c, :], ps)
                if first:
                    tap("qT", qT, [128, 4, TT], BF16)
                    tap("kT", kc[:, :, 0:TT], [128, 4, TT], BF16)
                    tap("us", us, [128, 8, TT])

                chk("C")
                nkb = 4 * t + 4

                def cyc(lst):
                    st_ = [0]

                    def f():
                        st_[0] += 1
                        return banks[lst[st_[0] % len(lst)]]
                    return f

                def attn_thread():
                    bkA = cyc([0, 1])
                    po, pz = banks[4], banks[5]
                    for h in range(4):
                        for m in range(2):
                            pb_ = m * 64

                            def S(kb):
                                q0 = 0 if kb < 4 * t else (kb - 4 * t) * 128
                                ps = bkA()
                                mm(ps[:, q0:TT], kc[pb_:pb_ + 64, h, kb * 128:(kb + 1) * 128], qT[pb_:pb_ + 64, h, q0:TT])
                                return ps, q0
                            nxt = S(0)
                            for kb in range(nkb):
                                ps, q0 = nxt
                                if kb + 1 < nkb:
                                    nxt = S(kb + 1)
                                p = pT[kb % 3]
                                act(p[:, q0:TT], ps[:, q0:TT], AF.Exp, scale=0.125)
                                if kb >= 4 * t:
                                    memset(p[64:128, q0:q0 + 64], 0.0)
                                mm(po[:, q0:TT], vc[:, kb, h * 128:(h + 1) * 128], p[:, q0:TT], kb == 0, kb == nkb - 1)
                                mm(pz[:, q0:TT], CB["ones"], p[:, q0:TT], kb == 0, kb == nkb - 1)
                                yield
                            a_ = at1 if m == 0 else at2
                            recip(a_, pz)
                            tt(a_, po, a_, ALU.mult)
                        stt(at3, at2, neglam, at1, ALU.mult, ALU.add)
                        act(sqb, at3, AF.Square)
                        ps = bkA()
                        mm(ps, o128, sqb)
                        rsq(rstd, ps, EPS)
                        stt(oaT[:, h, :], at3, subg, rstd, ALU.mult, ALU.mult)
                        yield
                    for hc in range(2):
                        n = 0
                        for e_ in range(2):
                            pb_ = e_ * 64
                            for mb in range(2):
                                ps = bkA()
                                mm(ps, kx[pb_:pb_ + 64, hc, mb * 128:(mb + 1) * 128], qxT[pb_:pb_ + 64, hc, :])
                                p = pT[n % 3]
                                act(p, ps, AF.Exp, scale=0.125)
                                mm(po, vx[:, mb * 4 + hc * 2 + e_, :], p, n == 0, n == 3)
                                mm(pz, CB["e0" if e_ == 0 else "e1"], p, n == 0, n == 3)
                                n += 1
                                yield
                        recip(at1, pz)
                        tt(ocT[:, hc, :], po, at1, ALU.mult)
                        yield

                ctx["bk"] = cyc([2, 3, 6])
                ga, gb = attn_thread(), rwkv_tile(ctx, first)
                na = 8 * nkb + 4 + 10
                nb_ = RW_UNITS_CONST
                da = db = 0
                alive_a = alive_b = True
                while alive_a or alive_b:
                    if alive_a and (not alive_b or da * nb_ <= db * na):
                        try:
                            next(ga)
                            da += 1
                            filler(FILL_A)
                        except StopIteration:
                            alive_a = False
                    elif alive_b:
                        try:
                            next(gb)
                            db += 1
                            filler(FILL_B)
                        except StopIteration:
                            alive_b = False
                if first:
                    tap("oaT", oaT, [128, 4, TT], BF16)
                    tap("ocT", ocT, [128, 2, TT], BF16)
                    tap("obT", obT, [128, 2, TT], BF16)
                chk("F")
                rmsnorm(xT, "mix_norm", hT)
                for c in range(8):
                    sa = wslot(sid("merge", 2 * c))
                    sb_ = wslot(sid("merge", 2 * c + 1))
                    gts = []
                    for br in range(3):
                        w_ = sa if br < 2 else sb_
                        o2 = (br % 2) * 1024
                        ps = bank()
                        for k in range(8):
                            mm(ps, w_[:, o2 + k * 128:o2 + (k + 1) * 128], hT[:, k, :], k == 0, k == 7)
                        g_ = (gb1, gb2, gb3)[br]
                        act(g_, ps, AF.Sigmoid, bias=cvc("gate_bias", br * 8 + c))
                        gts.append(g_)
                    o2 = 1024
                    ps = bank()
                    for k in range(4):
                        mm(ps, sb_[:, o2 + k * 128:o2 + (k + 1) * 128], oaT[:, k, :], k == 0, k == 3)
                    tt(tf, gts[0], ps, ALU.mult)
                    ps = bank()
                    for k in range(2):
                        mm(ps, sb_[:, o2 + 512 + k * 128:o2 + 512 + (k + 1) * 128], obT[:, k, :], k == 0, k == 1)
                    tt(ra, gts[1], ps, ALU.mult)
                    tt(tf, tf, ra, ALU.add)
                    ps = bank()
                    for k in range(2):
                        mm(ps, sb_[:, o2 + 768 + k * 128:o2 + 768 + (k + 1) * 128], ocT[:, k, :], k == 0, k == 1)
                    tt(ra, gts[2], ps, ALU.mult)
                    tt(mT[:, c, :], tf, ra, ALU.add)
                for j in range(4):
                    wo = wslot(sid("w_out", j))
                    for jj in range(2):
                        c = 2 * j + jj
                        ps = bank()
                        for k in range(8):
                            mm(ps, wo[:, k * 256 + jj * 128:k * 256 + jj * 128 + 128], mT[:, k, :], k == 0, k == 7)
                        tt(xT[:, c, :], ps, xT[:, c, :], ALU.add)
                if first:
                    tap("x2", xT, [128, 8, TT])

                chk("G")
                ffn("ffn2")
                chk("H")
                yT = us
                rmsnorm(xT, "final_norm", yT)
                for b in range(4):
                    yo = yout if b % 2 == 0 else xin_a
                    for c4 in range(2):
                        ps = bank()
                        for c in range(4):
                            tr(ps[:, c * 128:(c + 1) * 128], yT[:, c4 * 4 + c, b * 128:(b + 1) * 128])
                        evac(yo[:, c4 * 512:(c4 + 1) * 512], ps)
                    dma(dv(out_d[s, t0 + b * 128:t0 + (b + 1) * 128, :], "out_d", s * 1000 + t0 + b), yo, "pool")


    try:
        _body()
    except _Stop:
        pass
    P.emit()
    stack.close()
    nc._tap_names = tap_names
    return nc


def rwkv_tile(L, first):
    (us, RW, cs, cvc, cx, smw, tbd, obT, bank, banks, mm, tr, act, tt, ts, stt, cp, evac, memset, blk64, CB, P, tap) = (
        L[k] for k in ("us", "RW", "cs", "cvc", "cx", "smw", "tbd", "obT", "bank", "banks", "mm", "tr", "act",
                       "tt", "ts", "stt", "cp", "evac", "memset", "blk64", "CB", "P", "tap"))
    rsq = L["rsq"]
    chk = L["chk"]
    bk = L["bk"]
    blk = cs("blk")
    ident = cs("ident")
    sg, csg, E1, E2, E3, asig, kkn, kmod, thw, tmp2, vT = (RW[k] for k in (
        "sg", "csg", "E1", "E2", "E3", "asig", "kkn", "kmod", "thw", "tmp2", "vT"))
    Lm, Qm, Mt, L2, Q2, AKt, RBt, RKt = (RW[k] for k in ("L", "Q", "Mt", "L2", "Q2", "AKt", "RBt", "RKt"))
    Atok, Btok, Ktok, V0, V1, MAt, U0, U1 = (RW[k] for k in ("Atok", "Btok", "Ktok", "V0", "V1", "MAt", "U0", "U1"))
    mlow, mups, mupi = CB["mlow"][0:64], CB["mups"][0:64], CB["mupi"][0:64]
    bfv = L["bfv"]
    Atb, Btb = bfv(sg, 0), bfv(sg, 1)
    Ktb, Rtb = bfv(csg, 0), bfv(csg, 1)
    Lb, Mtb = bfv(Lm, 0), bfv(Lm, 1)
    Qb = bfv(Qm, 0)
    L2b, Q2b = bfv(L2, 0), bfv(Q2, 0)
    X = (7,)
    act(thw[0:64], us[0:64, 6, :], AF.Tanh)
    memset(U0, 0.0)
    memset(U1, 0.0)
    chk("r0")
    for hc in range(2):
        r_, k_, v_ = us[:, hc, :], us[:, 2 + hc, :], us[:, 4 + hc, :]
        ps = bk()
        mm(ps, smw[0:64, hc * 128:(hc + 1) * 128], thw[0:64])
        act(sg, ps, AF.Sigmoid, bias=cvc("rw_w0", hc))
        rst = CB["rst"]
        P.add("dve", lambda e, o=csg, m=rst, d=sg: e.tensor_tensor_scan(o.ap, m.ap, d.ap, 0.0, ALU.mult, ALU.add),
              reads=[rst, sg], writes=[csg])
        act(E1, csg, AF.Exp, scale=-C0)
        act(E2, csg, AF.Exp, scale=C0)
        tt(tmp2, csg, sg, ALU.subtract)
        act(E3, tmp2, AF.Exp, scale=-C0)
        yield
        chk("r1")
        ps = bk()
        mm(ps, smw[64:128, hc * 128:(hc + 1) * 128], us[64:128, 6, :])
        act(asig, ps, AF.Sigmoid, bias=cvc("rw_a0", hc))
        yield
        ts(kkn, k_, cvc("rw_k_k", hc), ALU.mult)
        act(tmp2, kkn, AF.Square)
        ps = bk()
        mm(ps, blk, tmp2)
        ts(tmp2, ps, 1e-24, ALU.max)
        rsq(tmp2, tmp2, 0.0)
        tt(kkn, kkn, tmp2, ALU.mult)
        yield
        ts(tmp2, asig, cvc("rw_k_a", hc), ALU.mult, cx[:, hc:hc + 1], ALU.add)
        tt(kmod, k_, tmp2, ALU.mult)
        stt(tmp2, r_, cvc("rw_r_k", hc), kmod, ALU.mult, ALU.mult)
        pbon = bk()
        mm(pbon, blk, tmp2)
        tt(vT, pbon, v_, ALU.mult)
        yield
        chk("r2")
        Rt, Kt, Bt, At = E1, kmod, E2, E3
        tt(Rt, E1, r_, ALU.mult)
        tt(Kt, kmod, E2, ALU.mult)
        tt(tmp2, kkn, asig, ALU.mult)
        tt(Bt, tmp2, E2, ALU.mult)
        stt(At, kkn, -1.0, E3, ALU.mult, ALU.mult)
        yield
        pcs = tmp2
        for c8 in range(8):
            act(pcs[:, c8:c8 + 1], csg[:, c8 * 64 + 63:c8 * 64 + 64], AF.Exp, scale=-C0)
        chk("r3")
        cp(Atb, At, "act")
        cp(Btb, Bt, "dve")
        cp(Ktb, Kt, "act")
        cp(Rtb, Rt, "dve")
        yield
        if first and hc == 0:
            tap("At", At, [128, TT])
            tap("Bt", Bt, [128, TT])
            tap("Kt", Kt, [128, TT])
            tap("Rt", Rt, [128, TT])
        py = banks[7]
        for half in range(2):
            memset(V0, 0.0)
            memset(V1, 0.0)
            ps = bk()
            for c in range(4):
                c8 = half * 4 + c
                tr(ps[0:64, c * 128:(c + 1) * 128], v_[:, c8 * 64:(c8 + 1) * 64])
            for c in range(4):
                evac(V0[:, c, 0:64], ps[0:64, c * 128:c * 128 + 64])
                evac(V1[:, c, 64:128], ps[0:64, c * 128 + 64:(c + 1) * 128])
            yield
            chk("r4")
            for src, dst in ((At, Atok), (Bt, Btok), (Kt, Ktok)):
                ps = bk()
                for c in range(4):
                    c8 = half * 4 + c
                    tr(ps[0:64, c * 128:(c + 1) * 128], src[:, c8 * 64:(c8 + 1) * 64])
                evac(dst, ps[0:64, :])
                yield
            chk("r5")

            def prod(dst, lh, rh, mask):
                ps = bk()
                for e_ in range(2):
                    for c in range(4):
                        c8 = half * 4 + c
                        pb_ = e_ * 64
                        col = (c * 2 + e_) * 64
                        mm(ps[0:64, col:col + 64], lh[pb_:pb_ + 64, c8 * 64:(c8 + 1) * 64],
                           rh[pb_:pb_ + 64, c8 * 64:(c8 + 1) * 64])
                chk("r5a")
                tt(dst, ps[0:64, :], mask, ALU.mult)
                yield
                chk("r5b")
            yield from prod(Lb, Atb, Btb, mlow)
            yield from prod(Qb, Btb, Atb, mups)
            yield from prod(AKt, Ktb, Atb, mups)
            yield from prod(RBt, Btb, Rtb, mupi)
            yield from prod(RKt, Ktb, Rtb, mupi)
            chk("r6")
            for j in range(8):
                tt(Mt[:, j * 64:(j + 1) * 64], Qb[:, j * 64:(j + 1) * 64], ident[0:64, 0:64], ALU.add)
            cp(Mtb, Mt, "dve")
            Lp, Qp, Ln, Qn = Lb, Qb, L2b, Q2b
            for step in range(5):
                ps = bk()
                for j in range(8):
                    mm(ps[0:64, j * 64:(j + 1) * 64], Qp[:, j * 64:(j + 1) * 64], Lp[:, j * 64:(j + 1) * 64])
                evac(Ln, ps[0:64, :])
                yield
                if step < 4:
                    ps = bk()
                    for j in range(8):
                        mm(ps[0:64, j * 64:(j + 1) * 64], Lp[:, j * 64:(j + 1) * 64], Qp[:, j * 64:(j + 1) * 64])
                    evac(Qn, ps[0:64, :])
                ps = bk()
                for j in range(8):
                    mm(ps[0:64, j * 64:(j + 1) * 64], Ln[:, j * 64:(j + 1) * 64], Mtb[:, j * 64:(j + 1) * 64])
                tt(Mt, ps[0:64, :], Mt, ALU.add)
                if step < 4:
                    cp(Mtb, Mt, "dve")
                yield
                Lp, Qp, Ln, Qn = Ln, Qn, Lp, Qp
            chk("r7")
            cp(Mtb, Mt, "dve")
            W1b = bfv(Qm, 0)
            G = L2
            Atokb = bfv(Qm, 1)
            cp(Atokb, Atok, "dve")
            ps = bk()
            for c in range(4):
                for e_ in range(2):
                    j = c * 2 + e_
                    Vp = V0 if e_ == 0 else V1
                    mm(ps[0:64, j * 64:(j + 1) * 64], AKt[:, j * 64:(j + 1) * 64], Vp[:, c, e_ * 64:e_ * 64 + 64])
            evac(W1b, ps[0:64, :])
            yield
            ps = bk()
            for j in range(8):
                mm(ps[0:64, j * 64:(j + 1) * 64], Mtb[:, j * 64:(j + 1) * 64], W1b[:, j * 64:(j + 1) * 64])
            evac(G, ps[0:64, :])
            yield
            for e_ in range(2):
                ps = bk()
                for c in range(4):
                    j = c * 2 + e_
                    mm(ps[:, c * 64:(c + 1) * 64], Atokb[:, c * 128:(c + 1) * 128], Mtb[:, j * 64:(j + 1) * 64])
                for c in range(4):
                    evac(MAt[e_ * 64:e_ * 64 + 64, c, :], ps[e_ * 64:e_ * 64 + 64, c * 64:(c + 1) * 64])
                yield
            chk("r8")
            for c in range(4):
                c8 = half * 4 + c
                pu = bk()
                for e_ in range(2):
                    pb_ = e_ * 64
                    mm(pu[0:64, e_ * 64:e_ * 64 + 64], MAt[pb_:pb_ + 64, c, :], tbd[pb_:pb_ + 64, hc, e_ * 64:e_ * 64 + 64])
                j0 = c * 2
                tt(U0[:, 0:64], pu[0:64, 0:64], G[:, j0 * 64:j0 * 64 + 64], ALU.add)
                tt(U1[:, 64:128], pu[0:64, 64:128], G[:, (j0 + 1) * 64:(j0 + 1) * 64 + 64], ALU.add)
                yield
                yc = py[:, c8 * 64:(c8 + 1) * 64]
                mm(yc, tbd[:, hc, :], Rt[:, c8 * 64:(c8 + 1) * 64], True, False)
                mm(yc, V0[:, c, :], RKt[:, j0 * 64:j0 * 64 + 64], False, False)
                mm(yc, V1[:, c, :], RKt[:, (j0 + 1) * 64:(j0 + 1) * 64 + 64], False, False)
                mm(yc, U0, RBt[:, j0 * 64:j0 * 64 + 64], False, False)
                mm(yc, U1, RBt[:, (j0 + 1) * 64:(j0 + 1) * 64 + 64], False, True)
                pt_ = bk()
                mm(pt_[:, 0:64], Ktok[:, c * 128:(c + 1) * 128], V0[:, c, 0:64], True, False)
                mm(pt_[:, 0:64], ident, tbd[:, hc, 0:64], False, False)
                mm(pt_[:, 0:64], Btok[:, c * 128:(c + 1) * 128], U0[:, 0:64], False, True)
                mm(pt_[:, 64:128], Ktok[:, c * 128:(c + 1) * 128], V1[:, c, 64:128], True, False)
                mm(pt_[:, 64:128], ident, tbd[:, hc, 64:128], False, False)
                mm(pt_[:, 64:128], Btok[:, c * 128:(c + 1) * 128], U1[:, 64:128], False, True)
                ts(tbd[0:64, hc, 0:64], pt_[0:64, 0:64], pcs[0:64, c8:c8 + 1], ALU.mult)
                ts(tbd[64:128, hc, 64:128], pt_[64:128, 64:128], pcs[64:128, c8:c8 + 1], ALU.mult)
                yield
                chk("r9")
        yT = asig
        cp(yT, py, "act")
        if first and hc == 0:
            tap("yraw", yT, [128, TT])
        pm_ = bk()
        mm(pm_, blk64, yT)
        tt(yT, yT, pm_, ALU.subtract)
        yield
        act(sg, yT, AF.Square)
        pv_ = bk()
        mm(pv_, blk64, sg)
        rsq(csg, pv_, GN_EPS)
        tt(yT, yT, csg, ALU.mult)
        ts(yT, yT, cvc("rw_ln_w", hc), ALU.mult, cvc("rw_ln_b", hc), ALU.add)
        tt(yT, yT, vT, ALU.add)
        yield
        act(sg, us[:, 7, :], AF.Sigmoid)
        pg = bk()
        mm(pg, smw[:, 256 + hc * 128:256 + (hc + 1) * 128], sg)
        tt(obT[:, hc, :], yT, pg, ALU.mult)


def _prep(inputs, nseq_per_core, ncores):
    x = np.asarray(inputs["x"], np.float32)
    mem = np.asarray(inputs["mem"], np.float32)
    pos = np.asarray(inputs["positions"], np.int32)
    wsl, idx = make_wslots(inputs)
    cst, cst2 = make_cst()
    cvec = make_cvec({k: np.asarray(v) for k, v in inputs.items()})
    smw = make_smallw(inputs)
    maps = []
    for c in range(ncores):
        sl = slice(c * nseq_per_core, (c + 1) * nseq_per_core)
        maps.append({"x": np.ascontiguousarray(x[sl]), "mem": np.ascontiguousarray(mem[sl]),
                     "pos": np.ascontiguousarray(pos[sl]), "cst": cst, "cst2": cst2, "cvec": cvec,
                     "smallw": smw, "wslots": wsl})
    return maps, idx, wsl.shape[0]


def kernel(**inputs):
    B, S, _ = inputs["x"].shape
    ncores = 8
    nseq = B // ncores
    maps, idx, nslots = _prep(inputs, nseq, ncores)
    nc = build(nseq, S, idx, nslots)
    res = run_bass_kernel_spmd(nc, maps, core_ids=list(range(ncores)))
    out = np.concatenate([np.asarray(r["out"]) for r in res.results], axis=0)
    return out.astype(np.float32)
```

```python
import math
import numpy as np
import concourse.bass as bass
import concourse.mybir as mybir
from concourse.bass_utils import run_bass_kernel_spmd

F32 = mybir.dt.float32
BF16 = mybir.dt.bfloat16
I32 = mybir.dt.int32
AF = mybir.ActivationFunctionType
ALU = mybir.AluOpType

D = 1024
DFF = 2816
TT = 512
CH = 64
MEM = 256
EPS = 1e-6
GN_EPS = 64e-5
LAM_INIT = 0.8 - 0.6 * math.exp(-0.3 * 0)
C0 = math.exp(-0.5)
C1_2PI = 6.28125
C2_2PI = 2 * math.pi - 6.28125
ESZ = {F32: 4, BF16: 2, I32: 4}
SLOT = 2048
RW_UNITS_CONST = 138
import os as _osf
FILL_A = int(_osf.environ.get('KFILL_A', 0))
FILL_B = int(_osf.environ.get('KFILL_B', 0))
NRING = 4


class V:
    def __init__(self, ap, space, esz, shape, strides, off, plo, phi):
        self.ap, self.space, self.esz = ap, space, esz
        self.shape, self.strides, self.off = list(shape), list(strides), off
        self.plo, self.phi = plo, phi

    def rng(self):
        if self.space.startswith("ps"):
            return (self.space, self.plo, self.phi, 0, 2048)
        hi = self.off + sum((n - 1) * s for n, s in zip(self.shape, self.strides)) + 1
        return (self.space, self.plo, self.phi, self.off * self.esz, hi * self.esz)

    def __getitem__(self, key):
        if not isinstance(key, tuple):
            key = (key,)
        pk = key[0]
        plo, phi = self.plo, self.phi
        if isinstance(pk, slice):
            a = 0 if pk.start is None else pk.start
            b = (self.phi - self.plo) if pk.stop is None else pk.stop
            plo, phi = self.plo + a, self.plo + b
        else:
            raise ValueError("partition index must be a slice")
        shape, strides, off = [], [], self.off
        fk = list(key[1:]) + [slice(None)] * (len(self.shape) - len(key) + 1)
        for k, n, s in zip(fk, self.shape, self.strides):
            if isinstance(k, slice):
                a = 0 if k.start is None else k.start
                b = n if k.stop is None else k.stop
                assert k.step is None and 0 <= a < b <= n, (k, n)
                off += a * s
                shape.append(b - a)
                strides.append(s)
            else:
                assert 0 <= k < n, (k, n)
                off += k * s
        return V(self.ap[key], self.space, self.esz, shape, strides, off, plo, phi)


class Op:
    __slots__ = ("eng", "idx", "fn", "deps", "ddeps", "signal", "dma", "ev", "dsem", "dval")

    def __init__(self, eng, idx, fn, dma):
        self.eng, self.idx, self.fn, self.dma = eng, idx, fn, dma
        self.deps = {}
        self.ddeps = []
        self.signal = False
        self.ev = None


ENGS = ("pe", "act", "dve", "pool", "sp")
NDSEM = {"sp": 6, "pool": 4, "act": 4}


class Prog:
    def __init__(self, nc):
        self.nc = nc
        self.ops = {e: [] for e in ENGS}
        self.spaces = {}
        self.ndma = {e: 0 for e in ENGS}
        self._bank = 0

    def _overl(self, r):
        sp, plo, phi, lo, hi = r
        lst = self.spaces.setdefault(sp, [])
        return lst, [e for e in lst if e[0] < phi and plo < e[1] and e[2] < hi and lo < e[3]]

    def _dep(self, op, prod, raw):
        if prod is None or prod is op:
            return
        if prod.dma:
            if prod not in op.ddeps:
                op.ddeps.append(prod)
            return
        if prod.eng == op.eng and not op.dma:
            if op.eng == "pe":
                return
        cur = op.deps.get(prod.eng)
        if cur is None or cur.idx < prod.idx:
            op.deps[prod.eng] = prod
        prod.signal = True

    def add(self, eng, fn, reads=(), writes=(), dma=False):
        op = Op(eng, len(self.ops[eng]), fn, dma)
        for v in reads:
            r = v.rng()
            lst, ov = self._overl(r)
            covered = False
            for e in ov:
                self._dep(op, e[4], True)
                e[5].append(op)
                if e[0] <= r[1] and e[1] >= r[2] and e[2] <= r[3] and e[3] >= r[4]:
                    covered = True
            if not covered:
                lst.append([r[1], r[2], r[3], r[4], None, [op]])
        for v in writes:
            r = v.rng()
            lst, ov = self._overl(r)
            for e in ov:
                self._dep(op, e[4], False)
                for rd in e[5]:
                    self._dep(op, rd, False)
                if r[1] <= e[0] and r[2] >= e[1] and r[3] <= e[2] and r[4] >= e[3]:
                    lst.remove(e)
            lst.append([r[1], r[2], r[3], r[4], op, []])
        self.ops[eng].append(op)
        return op

    def emit(self):
        nc = self.nc
        sems = []

        def newsem(name):
            s = nc.alloc_semaphore(name=name)
            sems.append(s)
            return s

        for eng in ("pe", "act", "dve", "pool"):
            cnt, sem, ep = 0, None, 0
            for op in self.ops[eng]:
                if op.dma or not op.signal:
                    continue
                if sem is None or cnt >= 30000:
                    sem = newsem(f"s_{eng}_{ep}")
                    ep += 1
                    cnt = 0
                cnt += 1
                op.ev = (sem, cnt)
        dsems = {}
        for eng in ENGS:
            k = 0
            for op in self.ops[eng]:
                if not op.dma:
                    continue
                n = NDSEM[eng]
                if (eng, k % n) not in dsems:
                    dsems[(eng, k % n)] = newsem(f"d_{eng}_{k % n}")
                op.dsem = dsems[(eng, k % n)]
                op.dval = 16 * (k // n + 1)
                k += 1

        def run(eng_name, e):
            seen = {}

            def wait(sem, val):
                if seen.get(id(sem), 0) >= val:
                    return
                e.wait_ge(sem, val)
                seen[id(sem)] = val

            last = {}
            for op in self.ops[eng_name]:
                for p in op.deps.values():
                    wait(*p.ev)
                for p in op.ddeps:
                    wait(p.dsem, p.dval)
                if op.dma:
                    if op.dval > 16:
                        wait(op.dsem, op.dval - 16)
                    op.fn(e).then_inc(op.dsem, 16)
                    last[id(op.dsem)] = (op.dsem, op.dval)
                else:
                    ins = op.fn(e)
                    if op.signal:
                        ins.then_inc(op.ev[0], 1)
            for sem, val in last.values():
                wait(sem, val)

        with nc.Block() as block:
            @block.tensor
            def _(e):
                run("pe", e)

            @block.scalar
            def _(e):
                run("act", e)

            @block.vector
            def _(e):
                run("dve", e)

            @block.gpsimd
            def _(e):
                run("pool", e)

            @block.sync
            def _(e):
                run("sp", e)


CST_COLS = {}
CST2_COLS = {}


def _cst_layout():
    off = 0
    for name, n in (("ident", 128), ("blk", 128), ("pm", 128), ("invf", 1), ("negpi", 1)):
        CST_COLS[name] = (off, n)
        off += n
    n1 = off
    off = 0
    for name, n in (("ones", 128), ("e0", 128), ("e1", 128), ("mlow", 512), ("mups", 512), ("mupi", 512),
                    ("rst", 512)):
        CST2_COLS[name] = (off, n)
        off += n
    return n1, off


NCST, NCST2 = _cst_layout()


def make_cst():
    c = np.zeros((128, NCST), np.float32)
    c2 = np.zeros((128, NCST2), np.float32)

    def put(name, arr):
        if name in CST_COLS:
            o, n = CST_COLS[name]
            c[:arr.shape[0], o:o + n] = arr
        else:
            o, n = CST2_COLS[name]
            c2[:arr.shape[0], o:o + n] = arr

    put("ident", np.eye(128, dtype=np.float32))
    blk = np.zeros((128, 128), np.float32)
    blk[:64, :64] = 1
    blk[64:, 64:] = 1
    put("blk", blk)
    pm = np.zeros((128, 128), np.float32)
    for g in range(2):
        for d in range(8):
            pm[g * 64 + d + 8, g * 64 + d] = -1.0
            pm[g * 64 + d, g * 64 + d + 8] = 1.0
    put("pm", pm)
    put("ones", np.ones((128, 128), np.float32))
    e0 = np.zeros((128, 128), np.float32)
    e0[:, :64] = 1
    e1 = np.zeros((128, 128), np.float32)
    e1[:, 64:] = 1
    put("e0", e0)
    put("e1", e1)
    i = np.arange(64)
    low = (i[None, :] < i[:, None]).astype(np.float32)
    ups = (i[:, None] < i[None, :]).astype(np.float32)
    upi = (i[:, None] <= i[None, :]).astype(np.float32)
    put("mlow", np.tile(low, (1, 8)))
    put("mups", np.tile(ups, (1, 8)))
    put("mupi", np.tile(upi, (1, 8)))
    rst = np.ones((128, 512), np.float32)
    rst[:, ::64] = 0
    put("rst", rst)
    invf = np.zeros((128, 1), np.float32)
    for g in range(2):
        for d in range(16):
            invf[g * 64 + d, 0] = np.float32(500000.0) ** np.float32(-(d % 8) * (2.0 / 16))
    put("invf", invf)
    put("negpi", np.full((128, 1), -math.pi, np.float32))
    return c, c2


CV = {}


def _cv_layout():
    off = 0
    for name, n in (("ffn1_norm", 8), ("mix_norm", 8), ("ffn2_norm", 8), ("final_norm", 8),
                    ("mem_norm", 8), ("gate_bias", 24), ("rw_mu", 8), ("rw_w0", 2), ("rw_a0", 2),
                    ("rw_k_k", 2), ("rw_k_a", 2), ("rw_r_k", 2), ("rw_ln_w", 2), ("rw_ln_b", 2),
                    ("da_subln", 1), ("lam", 4)):
        CV[name] = (off, n)
        off += n
    return off


NCV = _cv_layout()


def make_cvec(inp):
    c = np.zeros((128, NCV), np.float32)
    for name, (o, n) in CV.items():
        if name == "lam":
            for j, k in enumerate(("da_lambda_q1", "da_lambda_k1", "da_lambda_q2", "da_lambda_k2")):
                c[:64, o + j] = np.asarray(inp[k], np.float32).reshape(64)
        elif name == "da_subln":
            c[:, o] = np.asarray(inp[name], np.float32).reshape(128)
        else:
            c[:, o:o + n] = np.asarray(inp[name], np.float32).reshape(n, 128).T
    return c


def _colslots(w, c0, c1):
    out = []
    for c in range(c0, c1, 256):
        out.append(w[:, c:c + 256].reshape(8, 128, 256).transpose(1, 0, 2).reshape(128, 2048))
    return out


def _w2slots(w2):
    a = w2.reshape(22, 128, 8, 128).transpose(1, 2, 0, 3).reshape(128, 176, 128)
    return [a[:, s * 16:(s + 1) * 16, :].reshape(128, 2048) for s in range(11)]


def make_wslots(inp):
    g = lambda k: np.asarray(inp[k], np.float32)[0]
    slots = []
    idx = {}

    def reg(name, lst):
        idx[name] = (len(slots), len(lst))
        slots.extend(lst)

    for f in ("ffn1", "ffn2"):
        reg(f + "_w1", _colslots(g(f + "_w1"), 0, DFF))
        reg(f + "_w3", _colslots(g(f + "_w3"), 0, DFF))
        reg(f + "_w2", _w2slots(g(f + "_w2")))
    win = g("w_in")
    reg("w_qkvu", _colslots(win, 0, 2816))
    ua, ub, uc = g("w_up_a"), g("w_up_b"), g("w_up_c")
    mg = []
    for c in range(8):
        def gblk(br):
            col = 2816 + br * 1024 + c * 128
            return win[:, col:col + 128].reshape(8, 128, 128).transpose(1, 0, 2).reshape(128, 1024)
        sa = np.concatenate([gblk(0), gblk(1)], axis=1)
        ups = np.zeros((128, 1024), np.float32)
        o = 0
        for w_, nk in ((ua, 4), (ub, 2), (uc, 2)):
            for k in range(nk):
                ups[:, o:o + 128] = w_[k * 128:(k + 1) * 128, c * 128:(c + 1) * 128]
                o += 128
        sb_ = np.concatenate([gblk(2), ups], axis=1)
        mg.append(np.ascontiguousarray(sa))
        mg.append(np.ascontiguousarray(sb_))
    reg("merge", mg)
    reg("w_out", _colslots(g("w_out"), 0, 1024))
    reg("w_memkv", _colslots(g("w_mem_kv"), 0, 512))
    return np.ascontiguousarray(np.stack(slots)), idx


def make_smallw(inp):
    g = lambda k: np.asarray(inp[k], np.float32)[0]
    s = np.zeros((128, 512), np.float32)
    s[:64, :256] = g("rw_w2")
    s[64:, :256] = g("rw_a2")
    s[:, 256:512] = g("rw_g2")
    return s


_SLOT_IDX = None


class _Stop(Exception):
    pass


def build(NSEQ, S, slot_idx, nslots, taps=(), nt_limit=None, stop=None):
    import contextlib
    NT = S // TT
    NKB = S // 128
    nc = bass.Bass("TRN2", target_bir_lowering=False)
    P = Prog(nc)
    x_d = nc.dram_tensor("x", [NSEQ, S, D], F32, kind="ExternalInput").ap()
    mem_d = nc.dram_tensor("mem", [NSEQ, MEM, D], F32, kind="ExternalInput").ap()
    pos_d = nc.dram_tensor("pos", [NSEQ, S], I32, kind="ExternalInput").ap()
    cst_d = nc.dram_tensor("cst", [128, NCST], F32, kind="ExternalInput").ap()
    cst2_d = nc.dram_tensor("cst2", [128, NCST2], F32, kind="ExternalInput").ap()
    cv_d = nc.dram_tensor("cvec", [128, NCV], F32, kind="ExternalInput").ap()
    sw_d = nc.dram_tensor("smallw", [128, 512], F32, kind="ExternalInput").ap()
    wf_d = nc.dram_tensor("wslots", [nslots, 128, SLOT], F32, kind="ExternalInput").ap()
    wb_d = nc.dram_tensor("wbf", [nslots, 128, SLOT], BF16, kind="Internal").ap()
    out_d = nc.dram_tensor("out", [NSEQ, S, D], F32, kind="ExternalOutput").ap()
    stack = contextlib.ExitStack()

    def _strides(shape):
        st, s_ = [], 1
        for n in reversed(shape):
            st.insert(0, s_)
            s_ *= n
        return st

    def tile(name, shape, dt):
        t = stack.enter_context(nc.sbuf_tensor("sb_" + name, list(shape), dt))
        return V(t[:], name, ESZ[dt], shape[1:], _strides(shape[1:]), 0, 0, shape[0])

    def ptile(name):
        t = stack.enter_context(nc.psum_tensor(name, [128, 512], F32))
        return V(t[:], name, 4, [512], [1], 0, 0, 128)

    def dv(ap, name, idx=0):
        return V(ap, name, 1, [1], [1], idx, 0, 1)

    banks = [ptile(f"ps{i}") for i in range(8)]

    def bank(excl=()):
        for _ in range(9):
            P._bank = (P._bank + 1) % 8
            if P._bank not in excl:
                return banks[P._bank]

    PL = 46080
    ARENA = PL + 51200
    arena_t = stack.enter_context(nc.sbuf_tensor("arena", [128, ARENA // 4], F32))

    def carve(off, shape, dt):
        n = 1
        for k in shape[1:]:
            n *= k
        nb = n * ESZ[dt]
        assert off % 4 == 0 and nb % 4 == 0 and off + nb <= ARENA, (off, nb)
        ap = arena_t[:, off // 4:(off + nb) // 4]
        if dt != F32:
            ap = ap.bitcast(dt)
        if len(shape) == 3:
            ap = ap.rearrange("p (a b) -> p a b", a=shape[1])
        v = V(ap, "arena", ESZ[dt], shape[1:], _strides(shape[1:]), off // ESZ[dt], 0, 128)
        if shape[0] != 128:
            v = v[0:shape[0]]
        return v

    cst = tile("cst", [128, NCST], F32)
    NCB = 3 * 128 + 4 * 512
    cb = tile("cb", [128, NCB + 128], BF16)
    cv = tile("cv", [128, NCV], F32)
    smw = tile("smw", [128, 512], F32)
    cx = tile("cx", [128, 16], F32)
    blk64 = tile("blk64", [128, 128], F32)
    zt = tile("zt", [128, 128], F32)
    ring = [tile(f"ring{i}", [128, SLOT], BF16) for i in range(NRING)]
    kc = tile("kc", [128, 4, S], BF16)
    vc = tile("vc", [128, NKB, 512], BF16)
    kx = tile("kx", [128, 2, MEM], BF16)
    vx = tile("vx", [128, 8, 128], BF16)
    tbd = tile("tbd", [128, 2, 128], F32)
    xT = tile("xT", [128, 8, TT], F32)
    ucar = tile("ucar", [128, 8], F32)

    def cs(name):
        o, n = CST_COLS[name]
        return cst[:, o:o + n]

    def cvc(name, j=0):
        o, n = CV[name]
        return cv[:, o + j:o + j + 1]

    ident = cs("ident")
    epst = tile("epst", [128, 4], F32)
    EPSCOL = {EPS: 0, GN_EPS: 1, 0.0: 2}

    def eps_ap(e):
        return epst[:, EPSCOL[e]:EPSCOL[e] + 1]
    CB = {}
    for k_, (o_, n_) in CST2_COLS.items():
        CB[k_] = cb[:, o_:o_ + n_]
    CB["o1024"] = cb[:, NCB:NCB + 128]
    negpi = cs("negpi")

    last_rows = [None]

    def _rowguard(v, out=None, start=True):
        rows = (v.plo, v.phi)
        f32 = v.esz == 4
        prev = last_rows[0]
        if prev and rows != prev[:2] and rows != (0, 128) and prev[:2] != (0, 128):
            m_ = out.phi - out.plo
            o2 = out[:, 0:2]
            zl = zt[:, 0:m_]
            P.add("pe", lambda e: e.matmul(o2.ap, zl.ap, ident.ap[:, 0:2], start=start, stop=start),
                  reads=[zt] + ([] if start else [out]), writes=[out])
        last_rows[0] = rows + (f32,)

    def mm(out, lhsT, rhs, start=True, stop=True):
        _rowguard(lhsT, out, start)
        P.add("pe", lambda e: e.matmul(out.ap, lhsT.ap, rhs.ap, start=start, stop=stop),
              reads=[lhsT, rhs] + ([] if start else [out]), writes=[out])

    fill_rhs = cb[:, 0:512]

    def filler(n):
        for _ in range(n):
            last_rows[0] = (0, 128, False)
            dps = banks[3]
            P.add("pe", lambda e: e.matmul(dps.ap, CB["ones"].ap, fill_rhs.ap, start=True, stop=True), reads=[])

    def tr(out, in_):
        n = in_.phi - in_.plo
        idn = ident[0:n, 0:n]
        _rowguard(in_, out, True)
        P.add("pe", lambda e: e.transpose(out.ap, in_.ap, idn.ap), reads=[in_, idn], writes=[out])

    def act(out, in_, func, bias=None, scale=1.0):
        rd = [in_] + ([bias] if isinstance(bias, V) else [])
        b = bias.ap if isinstance(bias, V) else (0.0 if bias is None else bias)
        P.add("act", lambda e: e.activation(out.ap, in_.ap, func, bias=b, scale=scale), reads=rd, writes=[out])

    def tt(out, a, b, op, eng="dve"):
        P.add(eng, lambda e: e.tensor_tensor(out.ap, a.ap, b.ap, op), reads=[a, b], writes=[out])

    def ts(out, a, s1, op0, s2=None, op1=None, eng="dve"):
        rd = [a] + [s_ for s_ in (s1, s2) if isinstance(s_, V)]
        v1 = s1.ap if isinstance(s1, V) else s1
        v2 = s2.ap if isinstance(s2, V) else s2
        if op1 is None:
            P.add(eng, lambda e: e.tensor_scalar(out.ap, a.ap, v1, None, op0), reads=rd, writes=[out])
        else:
            P.add(eng, lambda e: e.tensor_scalar(out.ap, a.ap, v1, v2, op0, op1), reads=rd, writes=[out])

    def stt(out, a, s_, b, op0, op1):
        rd = [a, b] + ([s_] if isinstance(s_, V) else [])
        sv = s_.ap if isinstance(s_, V) else s_
        P.add("dve", lambda e: e.scalar_tensor_tensor(out.ap, a.ap, sv, b.ap, op0, op1), reads=rd, writes=[out])

    def cp(out, in_, eng="dve"):
        if eng == "act":
            P.add("act", lambda e: e.copy(out.ap, in_.ap), reads=[in_], writes=[out])
        else:
            P.add(eng, lambda e: e.tensor_copy(out.ap, in_.ap), reads=[in_], writes=[out])

    def recip(out, in_):
        P.add("dve", lambda e: e.reciprocal(out.ap, in_.ap), reads=[in_], writes=[out])

    def rsq(out, in_, eps):
        act(out, in_, AF.Sqrt, bias=eps_ap(eps))
        recip(out, out)

    def memset(out, val, eng="pool"):
        P.add(eng, lambda e: e.memset(out.ap, val), writes=[out])

    def dma(out, in_, q="sp"):
        P.add(q, lambda e: e.dma_start(out=out.ap, in_=in_.ap), reads=[in_], writes=[out], dma=True)

    evq = [0]

    def evac(out, in_):
        evq[0] ^= 1
        import os as _os2
        ev = _os2.environ.get("KEV")
        cp(out, in_, ev if ev else ("act" if int(in_.space[2:]) in (0, 2, 4) else "dve"))

    tap_done = set()
    tap_names = []

    def tap(name, v, shape, dt=F32):
        if name not in taps or name in tap_done:
            return
        tap_done.add(name)
        td = nc.dram_tensor("tap_" + name, list(shape), dt, kind="ExternalOutput").ap()
        tap_names.append("tap_" + name)
        dma(dv(td, "tap_" + name), v, "pool")

    wcnt = [0]

    def wslot(sid_):
        r = ring[wcnt[0] % NRING]
        wcnt[0] += 1
        dma(r, dv(wb_d[sid_], "wbf", sid_), "sp")
        return r

    def chk(stage):
        if stop == stage:
            raise _Stop()

    def sid(name, j=0):
        assert j < slot_idx[name][1]
        return slot_idx[name][0] + j

    sqb = carve(0, [128, TT], BF16)
    rstd = carve(1024, [128, TT], F32)
    silu_t = [carve(3072 + i * 2048, [128, TT], F32) for i in range(2)]
    gT = carve(7168, [128, 22, TT], BF16)
    mT = gT
    qT = carve(15360, [128, 4, TT], BF16)
    qxT = carve(19456, [128, 2, TT], BF16)
    oaT = carve(21504, [128, 4, TT], BF16)
    ocT = carve(25600, [128, 2, TT], BF16)
    obT = carve(27648, [128, 2, TT], BF16)
    us = carve(29696, [128, 8, TT], F32)
    hT = carve(PL, [128, 8, TT], BF16)
    xin = carve(PL + 8192, [128, D], F32)
    yout = carve(PL + 12288, [128, D], F32)
    o = PL + 16384
    posi = carve(o, [128, TT], I32); o += 2048
    ang = carve(o, [128, TT], F32); o += 2048
    cosT = carve(o, [128, TT], F32); o += 2048
    sinT = carve(o, [128, TT], F32); o += 2048
    tf = carve(o, [128, TT], F32); o += 2048
    ra = carve(o, [128, TT], F32); o += 2048
    rb = carve(o, [128, TT], F32); o += 2048
    u1 = carve(o, [128, TT + 1], F32); o += 2052
    o = PL + 34816
    tf_b = carve(o, [128, TT], F32); o += 2048
    ra_b = carve(o, [128, TT], F32); o += 2048
    rb_b = carve(o, [128, TT], F32); o += 2048
    u1_b = carve(o, [128, TT + 1], F32); o += 2052
    o = PL + 8192
    gb1 = carve(o + 3072, [128, TT], F32)
    gb2 = carve(o + 5120, [128, TT], F32)
    gb3 = carve(o + 7168, [128, TT], F32)
    o = 7168
    pT = [carve(o + i * 1024, [128, TT], BF16) for i in range(3)]; o += 3072
    at1 = carve(o, [128, TT], F32); o += 2048
    at2 = carve(o, [128, TT], F32); o += 2048
    at3 = silu_t[0]
    o = PL
    RW = {}
    for nm in ("sg", "csg", "E1", "E2", "E3", "asig", "kkn", "kmod", "thw", "tmp2", "vT"):
        RW[nm] = carve(o, [128, TT], F32); o += 2048
    for nm in ("L", "Q", "Mt", "L2", "Q2", "AKt", "RBt", "RKt", "Atok", "Btok", "Ktok"):
        RW[nm] = carve(o, [64, 512], F32); o += 2048
    RW["V0"] = carve(o, [64, 4, 128], F32); o += 2048
    RW["V1"] = carve(o, [64, 4, 128], F32); o += 2048
    RW["MAt"] = carve(o, [128, 4, 64], F32); o += 1024
    RW["U0"] = carve(o, [64, 128], F32); o += 512
    RW["U1"] = carve(o, [64, 128], F32); o += 512
    assert o <= ARENA, o

    def _body():
        dma(cst, dv(cst_d, "cst_d"))
        memset(epst[:, 0:1], EPS)
        memset(epst[:, 1:2], GN_EPS)
        memset(epst[:, 2:3], 0.0)
        memset(zt, 0.0)
        dma(cv, dv(cv_d, "cv_d"))
        dma(smw, dv(sw_d, "sw_d"))
        c2s = carve(PL + 24576, [128, NCST2], F32)
        dma(c2s, dv(cst2_d, "cst2_d"))
        cp(cb[:, 0:NCB], c2s)
        o_ = CST2_COLS["ones"][0]
        ts(CB["o1024"], c2s[:, o_:o_ + 128], 1.0 / 1024, ALU.mult)
        o128 = tile("o128", [128, 128], BF16)
        ts(o128, c2s[:, o_:o_ + 128], 1.0 / 128, ALU.mult)
        ka = CV["rw_k_a"][0]
        ts(cx[:, 0:2], cv[:, ka:ka + 2], -1.0, ALU.mult, 1.0, ALU.add)
        ts(cx[:, 2:3], cvc("da_subln"), 1.0 - LAM_INIT, ALU.mult)
        lo = CV["lam"][0]
        tt(cx[0:64, 4:5], cv[0:64, lo:lo + 1], cv[0:64, lo + 1:lo + 2], ALU.mult)
        tt(cx[0:64, 5:6], cv[0:64, lo + 2:lo + 3], cv[0:64, lo + 3:lo + 4], ALU.mult)
        b0 = bank()
        mm(b0[:, 0:2], c2s[0:64, o_:o_ + 128], cx[0:64, 4:6])
        act(cx[:, 6:8], b0[:, 0:2], AF.Exp)
        tt(cx[:, 8:9], cx[:, 7:8], cx[:, 6:7], ALU.subtract)
        ts(cx[:, 3:4], cx[:, 8:9], -LAM_INIT, ALU.add)
        neglam = cx[:, 3:4]
        subg = cx[:, 2:3]
        ts(blk64, cs("blk"), 1.0 / 64, ALU.mult)

        chk("pro0")
        stg_f = [carve(PL + i * 8192, [128, SLOT], F32) for i in range(3)]
        stg_b = [carve(PL + 36864 + i * 4096, [128, SLOT], BF16) for i in range(3)]
        import os as _os
        for s_ in range(int(_os.environ.get('KCAST0', 0)), int(_os.environ.get('KCAST', nslots))):
            f = stg_f[s_ % 3]
            b = stg_b[s_ % 3]
            dma(f, dv(wf_d[s_], "wf_d", s_), "sp")
            cp(b, f, "dve" if s_ % 2 == 0 else "act")
            dma(dv(wb_d[s_], "wbf", s_), b, "pool" if s_ % 2 == 0 else "act")
        chk("pro")
        sqs = [sqb, carve(5120, [128, TT], BF16), carve(6144, [128, TT], BF16)]

        def rmsnorm(src, gname, dst, n=TT):
            ps = bank()
            for c in range(8):
                sq_ = sqs[c % 3]
                act(sq_[:, 0:n], src[:, c, 0:n], AF.Square)
                mm(ps[:, 0:n], CB["o1024"], sq_[:, 0:n], start=(c == 0), stop=(c == 7))
            rsq(rstd[:, 0:n], ps[:, 0:n], EPS)
            for c in range(8):
                stt(dst[:, c, 0:n], src[:, c, 0:n], cvc(gname, c), rstd[:, 0:n], ALU.mult, ALU.mult)

        def ffn(pref):
            rmsnorm(xT, pref + "_norm", hT)
            for j in range(11):
                w1 = wslot(sid(pref + "_w1", j))
                w3 = wslot(sid(pref + "_w3", j))
                for jj in range(2):
                    pa, pb = bank(), bank()
                    for k in range(8):
                        mm(pa, w1[:, k * 256 + jj * 128:k * 256 + jj * 128 + 128], hT[:, k, :], k == 0, k == 7)
                    for k in range(8):
                        mm(pb, w3[:, k * 256 + jj * 128:k * 256 + jj * 128 + 128], hT[:, k, :], k == 0, k == 7)
                    st = silu_t[(2 * j + jj) % 2]
                    act(st, pa, AF.Silu)
                    tt(gT[:, 2 * j + jj, :], st, pb, ALU.mult)
            pair = 0
            w2 = None
            for c in range(8):
                ps = bank()
                for k in range(22):
                    if pair % 16 == 0:
                        w2 = wslot(sid(pref + "_w2", pair // 16))
                    o2 = (pair % 16) * 128
                    mm(ps, w2[:, o2:o2 + 128], gT[:, k, :], k == 0, k == 21)
                    pair += 1
                stt(xT[:, c, :], ps, 0.5, xT[:, c, :], ALU.mult, ALU.add)

        ldq = [0]
        xin_a = xin

        def load_T(src_ap, name, col0):
            ldq[0] ^= 1
            xin = xin_a if ldq[0] else yout
            dma(xin, dv(src_ap, name))
            chk("l0")
            for c4 in range(2):
                ps = bank()
                for c in range(4):
                    tr(ps[:, c * 128:(c + 1) * 128], xin[:, (c4 * 4 + c) * 128:(c4 * 4 + c + 1) * 128])
                    chk("l1")
                chk("l2")
                for c in range(4):
                    evac(xT[:, c4 * 4 + c, col0:col0 + 128], ps[:, c * 128:(c + 1) * 128])
                    chk("l3")
                    if c == 1:
                        chk("l3b")
                    if c == 2:
                        chk("l3c")
                chk("l4")

        def bfv(v, half):
            off_b = v.off * 4 + half * 1024
            return carve(off_b, [v.phi - v.plo, TT], BF16)

        ctx = dict(bfv=bfv, us=us, RW=RW, cs=cs, cvc=cvc, cx=cx, smw=smw, tbd=tbd, obT=obT, bank=bank, banks=banks, mm=mm,
                   tr=tr, act=act, tt=tt, ts=ts, stt=stt, cp=cp, evac=(lambda o_, i_: cp(o_, i_, "dve")), memset=memset, blk64=blk64, CB=CB,
                   P=P, tap=tap, rsq=rsq, chk=chk)

        for s in range(NSEQ):
            memset(tbd, 0.0)
            memset(ucar, 0.0)
            memset(vx, 0.0)
            chk("m0")
            for b in range(2):
                load_T(mem_d[s, b * 128:(b + 1) * 128, :], "mem_d", b * 128)
            chk("m1")
            rmsnorm(xT, "mem_norm", hT, n=256)
            chk("m2")
            wk = wslot(sid("w_memkv", 0))
            for hc in range(2):
                ps = bank()
                for k in range(8):
                    mm(ps[:, 0:256], wk[:, k * 256 + hc * 128:k * 256 + hc * 128 + 128], hT[:, k, 0:256], k == 0, k == 7)
                evac(kx[:, hc, :], ps[:, 0:256])
            chk("m3")
            wv = wslot(sid("w_memkv", 1))
            for mb in range(2):
                ps = bank()
                for k in range(8):
                    mm(ps[:, 0:256], hT[:, k, mb * 128:(mb + 1) * 128], wv[:, k * 256:(k + 1) * 256], k == 0, k == 7)
                for h in range(4):
                    e_ = h % 2
                    evac(vx[:, mb * 4 + (h // 2) * 2 + e_, e_ * 64:e_ * 64 + 64], ps[:, h * 64:(h + 1) * 64])

            chk("mem")
            for t in range(NT if nt_limit is None else nt_limit):
                t0 = t * TT
                first = (s == 0 and t == 0)
                for b in range(4):
                    load_T(x_d[s, t0 + b * 128:t0 + (b + 1) * 128, :], "x_d", b * 128)
                chk("A")
                ffn("ffn1")
                if first:
                    tap("x1", xT, [128, 8, TT])
                chk("B")
                rmsnorm(xT, "mix_norm", hT)
                dma(posi, dv(pos_d[s, t0:t0 + TT].partition_broadcast(128), "pos_d"))
                cp(ang, posi)
                ts(ang, ang, cs("invf"), ALU.mult)
                for dst_, shift in ((sinT, 0.0), (cosT, 0.5 * math.pi)):
                    if shift:
                        ts(rb, ang, shift, ALU.add)
                        a_ = rb
                    else:
                        a_ = ang
                    ts(ra, a_, 1.0 / (2 * math.pi), ALU.mult)
                    cp(posi, ra)
                    cp(ra, posi)
                    stt(tf, ra, -C1_2PI, a_, ALU.mult, ALU.add)
                    stt(tf, ra, -C2_2PI, tf, ALU.mult, ALU.add)
                    ts(ra, tf, math.pi, ALU.is_gt)
                    stt(tf, ra, -2 * math.pi, tf, ALU.mult, ALU.add)
                    ts(ra, tf, -math.pi, ALU.is_lt)
                    stt(tf, ra, 2 * math.pi, tf, ALU.mult, ALU.add)
                    act(dst_, tf, AF.Sin)

                rq = [0]

                def rope_chunk(w, jj, dst):
                    rq[0] ^= 1
                    tf_, ra_, rb_ = (tf, ra, rb) if rq[0] else (tf_b, ra_b, rb_b)
                    ps = bank()
                    for k in range(8):
                        mm(ps, w[:, k * 256 + jj * 128:k * 256 + jj * 128 + 128], hT[:, k, :], k == 0, k == 7)
                    cp(tf_, ps, "act")
                    pr = bank()
                    mm(pr, cs("pm"), tf_)
                    tt(ra_, tf_, cosT, ALU.mult)
                    tt(rb_, pr, sinT, ALU.mult)
                    tt(dst, ra_, rb_, ALU.add)

                wq = [wslot(sid("w_qkvu", j)) for j in range(2)]
                for c in range(4):
                    rope_chunk(wq[c // 2], c % 2, qT[:, c, :])
                wkk = [wslot(sid("w_qkvu", 2 + j)) for j in range(2)]
                for c in range(4):
                    rope_chunk(wkk[c // 2], c % 2, kc[:, c, t0:t0 + TT])
                wvv = [wslot(sid("w_qkvu", 4 + j)) for j in range(2)]
                for b in range(4):
                    ps = bank()
                    for j in range(2):
                        for k in range(8):
                            mm(ps[:, j * 256:(j + 1) * 256], hT[:, k, b * 128:(b + 1) * 128],
                               wvv[j][:, k * 256:(k + 1) * 256], k == 0, k == 7)
                    evac(vc[:, t * 4 + b, :], ps)
                for j in range(4):
                    wu = wslot(sid("w_qkvu", 6 + j))
                    for jj in range(2):
                        c = 2 * j + jj
                        ps = bank()
                        for k in range(8):
                            mm(ps, wu[:, k * 256 + jj * 128:k * 256 + jj * 128 + 128], hT[:, k, :], k == 0, k == 7)
                        u1_, tf_ = (u1, tf) if c % 2 == 0 else (u1_b, tf_b)
                        cp(u1_[:, 0:1], ucar[:, c:c + 1], "dve")
                        cp(u1_[:, 1:TT + 1], ps, "act")
                        cp(ucar[:, c:c + 1], u1_[:, TT:TT + 1], "dve")
                        tt(tf_, u1_[:, 0:TT], u1_[:, 1:TT + 1], ALU.subtract)
                        stt(us[:, c, :], tf_, cvc("rw_mu", c), u1_[:, 1:TT + 1], ALU.mult, ALU.add)
                wqx = wslot(sid("w_qkvu", 10))
                for c in range(2):
                    ps = bank()
                    for k in range(8):
                        mm(ps, wqx[:, k * 256 + c * 128:k * 256 + c * 128 + 128], hT[:, k, :], k == 0, k == 7)
                    evac(qxT[:, c, :], ps)
                if first:
                    tap("qT", qT, [128, 4, TT], BF16)
                    tap("kT", kc[:, :, 0:TT], [128, 4, TT], BF16)
                    tap("us", us, [128, 8, TT])

                chk("C")
                nkb = 4 * t + 4

                def cyc(lst):
                    st_ = [0]

                    def f():
                        st_[0] += 1
                        return banks[lst[st_[0] % len(lst)]]
                    return f

                def attn_thread():
                    bkA = cyc([0, 1])
                    po, pz = banks[4], banks[5]
                    for h in range(4):
                        for m in range(2):
                            pb_ = m * 64

                            def S(kb):
                                q0 = 0 if kb < 4 * t else (kb - 4 * t) * 128
                                ps = bkA()
                                mm(ps[:, q0:TT], kc[pb_:pb_ + 64, h, kb * 128:(kb + 1) * 128], qT[pb_:pb_ + 64, h, q0:TT])
                                return ps, q0
                            nxt = S(0)
                            for kb in range(nkb):
                                ps, q0 = nxt
                                if kb + 1 < nkb:
                                    nxt = S(kb + 1)
                                p = pT[kb % 3]
                                act(p[:, q0:TT], ps[:, q0:TT], AF.Exp, scale=0.125)
                                if kb >= 4 * t:
                                    memset(p[64:128, q0:q0 + 64], 0.0)
                                mm(po[:, q0:TT], vc[:, kb, h * 128:(h + 1) * 128], p[:, q0:TT], kb == 0, kb == nkb - 1)
                                mm(pz[:, q0:TT], CB["ones"], p[:, q0:TT], kb == 0, kb == nkb - 1)
                                yield
                            a_ = at1 if m == 0 else at2
                            recip(a_, pz)
                            tt(a_, po, a_, ALU.mult)
                        stt(at3, at2, neglam, at1, ALU.mult, ALU.add)
                        act(sqb, at3, AF.Square)
                        ps = bkA()
                        mm(ps, o128, sqb)
                        rsq(rstd, ps, EPS)
                        stt(oaT[:, h, :], at3, subg, rstd, ALU.mult, ALU.mult)
                        yield
                    for hc in range(2):
                        n = 0
                        for e_ in range(2):
                            pb_ = e_ * 64
                            for mb in range(2):
                                ps = bkA()
                                mm(ps, kx[pb_:pb_ + 64, hc, mb * 128:(mb + 1) * 128], qxT[pb_:pb_ + 64, hc, :])
                                p = pT[n % 3]
                                act(p, ps, AF.Exp, scale=0.125)
                                mm(po, vx[:, mb * 4 + hc * 2 + e_, :], p, n == 0, n == 3)
                                mm(pz, CB["e0" if e_ == 0 else "e1"], p, n == 0, n == 3)
                                n += 1
                                yield
                        recip(at1, pz)
                        tt(ocT[:, hc, :], po, at1, ALU.mult)
                        yield

                ctx["bk"] = cyc([2, 3, 6])
                ga, gb = attn_thread(), rwkv_tile(ctx, first)
                na = 8 * nkb + 4 + 10
                nb_ = RW_UNITS_CONST
                da = db = 0
                alive_a = alive_b = True
                while alive_a or alive_b:
                    if alive_a and (not alive_b or da * nb_ <= db * na):
                        try:
                            next(ga)
                            da += 1
                            filler(FILL_A)
                        except StopIteration:
                            alive_a = False
                    elif alive_b:
                        try:
                            next(gb)
                            db += 1
                            filler(FILL_B)
                        except StopIteration:
                            alive_b = False
                if first:
                    tap("oaT", oaT, [128, 4, TT], BF16)
                    tap("ocT", ocT, [128, 2, TT], BF16)
                    tap("obT", obT, [128, 2, TT], BF16)
                chk("F")
                rmsnorm(xT, "mix_norm", hT)
                for c in range(8):
                    sa = wslot(sid("merge", 2 * c))
                    sb_ = wslot(sid("merge", 2 * c + 1))
                    gts = []
                    for br in range(3):
                        w_ = sa if br < 2 else sb_
                        o2 = (br % 2) * 1024
                        ps = bank()
                        for k in range(8):
                            mm(ps, w_[:, o2 + k * 128:o2 + (k + 1) * 128], hT[:, k, :], k == 0, k == 7)
                        g_ = (gb1, gb2, gb3)[br]
                        act(g_, ps, AF.Sigmoid, bias=cvc("gate_bias", br * 8 + c))
                        gts.append(g_)
                    o2 = 1024
                    ps = bank()
                    for k in range(4):
                        mm(ps, sb_[:, o2 + k * 128:o2 + (k + 1) * 128], oaT[:, k, :], k == 0, k == 3)
                    tt(tf, gts[0], ps, ALU.mult)
                    ps = bank()
                    for k in range(2):
                        mm(ps, sb_[:, o2 + 512 + k * 128:o2 + 512 + (k + 1) * 128], obT[:, k, :], k == 0, k == 1)
                    tt(ra, gts[1], ps, ALU.mult)
                    tt(tf, tf, ra, ALU.add)
                    ps = bank()
                    for k in range(2):
                        mm(ps, sb_[:, o2 + 768 + k * 128:o2 + 768 + (k + 1) * 128], ocT[:, k, :], k == 0, k == 1)
                    tt(ra, gts[2], ps, ALU.mult)
                    tt(mT[:, c, :], tf, ra, ALU.add)
                for j in range(4):
                    wo = wslot(sid("w_out", j))
                    for jj in range(2):
                        c = 2 * j + jj
                        ps = bank()
                        for k in range(8):
                            mm(ps, wo[:, k * 256 + jj * 128:k * 256 + jj * 128 + 128], mT[:, k, :], k == 0, k == 7)
                        tt(xT[:, c, :], ps, xT[:, c, :], ALU.add)
                if first:
                    tap("x2", xT, [128, 8, TT])

                chk("G")
                ffn("ffn2")
                chk("H")
                yT = us
                rmsnorm(xT, "final_norm", yT)
                for b in range(4):
                    yo = yout if b % 2 == 0 else xin_a
                    for c4 in range(2):
                        ps = bank()
                        for c in range(4):
                            tr(ps[:, c * 128:(c + 1) * 128], yT[:, c4 * 4 + c, b * 128:(b + 1) * 128])
                        evac(yo[:, c4 * 512:(c4 + 1) * 512], ps)
                    dma(dv(out_d[s, t0 + b * 128:t0 + (b + 1) * 128, :], "out_d", s * 1000 + t0 + b), yo, "pool")


    try:
        _body()
    except _Stop:
        pass
    P.emit()
    stack.close()
    nc._tap_names = tap_names
    return nc


def rwkv_tile(L, first):
    (us, RW, cs, cvc, cx, smw, tbd, obT, bank, banks, mm, tr, act, tt, ts, stt, cp, evac, memset, blk64, CB, P, tap) = (
        L[k] for k in ("us", "RW", "cs", "cvc", "cx", "smw", "tbd", "obT", "bank", "banks", "mm", "tr", "act",
                       "tt", "ts", "stt", "cp", "evac", "memset", "blk64", "CB", "P", "tap"))
    rsq = L["rsq"]
    chk = L["chk"]
    bk = L["bk"]
    blk = cs("blk")
    ident = cs("ident")
    sg, csg, E1, E2, E3, asig, kkn, kmod, thw, tmp2, vT = (RW[k] for k in (
        "sg", "csg", "E1", "E2", "E3", "asig", "kkn", "kmod", "thw", "tmp2", "vT"))
    Lm, Qm, Mt, L2, Q2, AKt, RBt, RKt = (RW[k] for k in ("L", "Q", "Mt", "L2", "Q2", "AKt", "RBt", "RKt"))
    Atok, Btok, Ktok, V0, V1, MAt, U0, U1 = (RW[k] for k in ("Atok", "Btok", "Ktok", "V0", "V1", "MAt", "U0", "U1"))
    mlow, mups, mupi = CB["mlow"][0:64], CB["mups"][0:64], CB["mupi"][0:64]
    bfv = L["bfv"]
    Atb, Btb = bfv(sg, 0), bfv(sg, 1)
    Ktb, Rtb = bfv(csg, 0), bfv(csg, 1)
    Lb, Mtb = bfv(Lm, 0), bfv(Lm, 1)
    Qb = bfv(Qm, 0)
    L2b, Q2b = bfv(L2, 0), bfv(Q2, 0)
    X = (7,)
    act(thw[0:64], us[0:64, 6, :], AF.Tanh)
    memset(U0, 0.0)
    memset(U1, 0.0)
    chk("r0")
    for hc in range(2):
        r_, k_, v_ = us[:, hc, :], us[:, 2 + hc, :], us[:, 4 + hc, :]
        ps = bk()
        mm(ps, smw[0:64, hc * 128:(hc + 1) * 128], thw[0:64])
        act(sg, ps, AF.Sigmoid, bias=cvc("rw_w0", hc))
        rst = CB["rst"]
        P.add("dve", lambda e, o=csg, m=rst, d=sg: e.tensor_tensor_scan(o.ap, m.ap, d.ap, 0.0, ALU.mult, ALU.add),
              reads=[rst, sg], writes=[csg])
        act(E1, csg, AF.Exp, scale=-C0)
        act(E2, csg, AF.Exp, scale=C0)
        tt(tmp2, csg, sg, ALU.subtract)
        act(E3, tmp2, AF.Exp, scale=-C0)
        yield
        chk("r1")
        ps = bk()
        mm(ps, smw[64:128, hc * 128:(hc + 1) * 128], us[64:128, 6, :])
        act(asig, ps, AF.Sigmoid, bias=cvc("rw_a0", hc))
        yield
        ts(kkn, k_, cvc("rw_k_k", hc), ALU.mult)
        act(tmp2, kkn, AF.Square)
        ps = bk()
        mm(ps, blk, tmp2)
        ts(tmp2, ps, 1e-24, ALU.max)
        rsq(tmp2, tmp2, 0.0)
        tt(kkn, kkn, tmp2, ALU.mult)
        yield
        ts(tmp2, asig, cvc("rw_k_a", hc), ALU.mult, cx[:, hc:hc + 1], ALU.add)
        tt(kmod, k_, tmp2, ALU.mult)
        stt(tmp2, r_, cvc("rw_r_k", hc), kmod, ALU.mult, ALU.mult)
        pbon = bk()
        mm(pbon, blk, tmp2)
        tt(vT, pbon, v_, ALU.mult)
        yield
        chk("r2")
        Rt, Kt, Bt, At = E1, kmod, E2, E3
        tt(Rt, E1, r_, ALU.mult)
        tt(Kt, kmod, E2, ALU.mult)
        tt(tmp2, kkn, asig, ALU.mult)
        tt(Bt, tmp2, E2, ALU.mult)
        stt(At, kkn, -1.0, E3, ALU.mult, ALU.mult)
        yield
        pcs = tmp2
        for c8 in range(8):
            act(pcs[:, c8:c8 + 1], csg[:, c8 * 64 + 63:c8 * 64 + 64], AF.Exp, scale=-C0)
        chk("r3")
        cp(Atb, At, "act")
        cp(Btb, Bt, "dve")
        cp(Ktb, Kt, "act")
        cp(Rtb, Rt, "dve")
        yield
        if first and hc == 0:
            tap("At", At, [128, TT])
            tap("Bt", Bt, [128, TT])
            tap("Kt", Kt, [128, TT])
            tap("Rt", Rt, [128, TT])
        py = banks[7]
        for half in range(2):
            memset(V0, 0.0)
            memset(V1, 0.0)
            ps = bk()
            for c in range(4):
                c8 = half * 4 + c
                tr(ps[0:64, c * 128:(c + 1) * 128], v_[:, c8 * 64:(c8 + 1) * 64])
            for c in range(4):
                evac(V0[:, c, 0:64], ps[0:64, c * 128:c * 128 + 64])
                evac(V1[:, c, 64:128], ps[0:64, c * 128 + 64:(c + 1) * 128])
            yield
            chk("r4")
            for src, dst in ((At, Atok), (Bt, Btok), (Kt, Ktok)):
                ps = bk()
                for c in range(4):
                    c8 = half * 4 + c
                    tr(ps[0:64, c * 128:(c + 1) * 128], src[:, c8 * 64:(c8 + 1) * 64])
                evac(dst, ps[0:64, :])
                yield
            chk("r5")

            def prod(dst, lh, rh, mask):
                ps = bk()
                for e_ in range(2):
                    for c in range(4):
                        c8 = half * 4 + c
                        pb_ = e_ * 64
                        col = (c * 2 + e_) * 64
                        mm(ps[0:64, col:col + 64], lh[pb_:pb_ + 64, c8 * 64:(c8 + 1) * 64],
                           rh[pb_:pb_ + 64, c8 * 64:(c8 + 1) * 64])
                chk("r5a")
                tt(dst, ps[0:64, :], mask, ALU.mult)
                yield
                chk("r5b")
            yield from prod(Lb, Atb, Btb, mlow)
            yield from prod(Qb, Btb, Atb, mups)
            yield from prod(AKt, Ktb, Atb, mups)
            yield from prod(RBt, Btb, Rtb, mupi)
            yield from prod(RKt, Ktb, Rtb, mupi)
            chk("r6")
            for j in range(8):
                tt(Mt[:, j * 64:(j + 1) * 64], Qb[:, j * 64:(j + 1) * 64], ident[0:64, 0:64], ALU.add)
            cp(Mtb, Mt, "dve")
            Lp, Qp, Ln, Qn = Lb, Qb, L2b, Q2b
            for step in range(5):
                ps = bk()
                for j in range(8):
                    mm(ps[0:64, j * 64:(j + 1) * 64], Qp[:, j * 64:(j + 1) * 64], Lp[:, j * 64:(j + 1) * 64])
                evac(Ln, ps[0:64, :])
                yield
                if step < 4:
                    ps = bk()
                    for j in range(8):
                        mm(ps[0:64, j * 64:(j + 1) * 64], Lp[:, j * 64:(j + 1) * 64], Qp[:, j * 64:(j + 1) * 64])
                    evac(Qn, ps[0:64, :])
                ps = bk()
                for j in range(8):
                    mm(ps[0:64, j * 64:(j + 1) * 64], Ln[:, j * 64:(j + 1) * 64], Mtb[:, j * 64:(j + 1) * 64])
                tt(Mt, ps[0:64, :], Mt, ALU.add)
                if step < 4:
                    cp(Mtb, Mt, "dve")
                yield
                Lp, Qp, Ln, Qn = Ln, Qn, Lp, Qp
            chk("r7")
            cp(Mtb, Mt, "dve")
            W1b = bfv(Qm, 0)
            G = L2
            Atokb = bfv(Qm, 1)
            cp(Atokb, Atok, "dve")
            ps = bk()
            for c in range(4):
                for e_ in range(2):
                    j = c * 2 + e_
                    Vp = V0 if e_ == 0 else V1
                    mm(ps[0:64, j * 64:(j + 1) * 64], AKt[:, j * 64:(j + 1) * 64], Vp[:, c, e_ * 64:e_ * 64 + 64])
            evac(W1b, ps[0:64, :])
            yield
            ps = bk()
            for j in range(8):
                mm(ps[0:64, j * 64:(j + 1) * 64], Mtb[:, j * 64:(j + 1) * 64], W1b[:, j * 64:(j + 1) * 64])
            evac(G, ps[0:64, :])
            yield
            for e_ in range(2):
                ps = bk()
                for c in range(4):
                    j = c * 2 + e_
                    mm(ps[:, c * 64:(c + 1) * 64], Atokb[:, c * 128:(c + 1) * 128], Mtb[:, j * 64:(j + 1) * 64])
                for c in range(4):
                    evac(MAt[e_ * 64:e_ * 64 + 64, c, :], ps[e_ * 64:e_ * 64 + 64, c * 64:(c + 1) * 64])
                yield
            chk("r8")
            for c in range(4):
                c8 = half * 4 + c
                pu = bk()
                for e_ in range(2):
                    pb_ = e_ * 64
                    mm(pu[0:64, e_ * 64:e_ * 64 + 64], MAt[pb_:pb_ + 64, c, :], tbd[pb_:pb_ + 64, hc, e_ * 64:e_ * 64 + 64])
                j0 = c * 2
                tt(U0[:, 0:64], pu[0:64, 0:64], G[:, j0 * 64:j0 * 64 + 64], ALU.add)
                tt(U1[:, 64:128], pu[0:64, 64:128], G[:, (j0 + 1) * 64:(j0 + 1) * 64 + 64], ALU.add)
                yield
                yc = py[:, c8 * 64:(c8 + 1) * 64]
                mm(yc, tbd[:, hc, :], Rt[:, c8 * 64:(c8 + 1) * 64], True, False)
                mm(yc, V0[:, c, :], RKt[:, j0 * 64:j0 * 64 + 64], False, False)
                mm(yc, V1[:, c, :], RKt[:, (j0 + 1) * 64:(j0 + 1) * 64 + 64], False, False)
                mm(yc, U0, RBt[:, j0 * 64:j0 * 64 + 64], False, False)
                mm(yc, U1, RBt[:, (j0 + 1) * 64:(j0 + 1) * 64 + 64], False, True)
                pt_ = bk()
                mm(pt_[:, 0:64], Ktok[:, c * 128:(c + 1) * 128], V0[:, c, 0:64], True, False)
                mm(pt_[:, 0:64], ident, tbd[:, hc, 0:64], False, False)
                mm(pt_[:, 0:64], Btok[:, c * 128:(c + 1) * 128], U0[:, 0:64], False, True)
                mm(pt_[:, 64:128], Ktok[:, c * 128:(c + 1) * 128], V1[:, c, 64:128], True, False)
                mm(pt_[:, 64:128], ident, tbd[:, hc, 64:128], False, False)
                mm(pt_[:, 64:128], Btok[:, c * 128:(c + 1) * 128], U1[:, 64:128], False, True)
                ts(tbd[0:64, hc, 0:64], pt_[0:64, 0:64], pcs[0:64, c8:c8 + 1], ALU.mult)
                ts(tbd[64:128, hc, 64:128], pt_[64:128, 64:128], pcs[64:128, c8:c8 + 1], ALU.mult)
                yield
                chk("r9")
        yT = asig
        cp(yT, py, "act")
        if first and hc == 0:
            tap("yraw", yT, [128, TT])
        pm_ = bk()
        mm(pm_, blk64, yT)
        tt(yT, yT, pm_, ALU.subtract)
        yield
        act(sg, yT, AF.Square)
        pv_ = bk()
        mm(pv_, blk64, sg)
        rsq(csg, pv_, GN_EPS)
        tt(yT, yT, csg, ALU.mult)
        ts(yT, yT, cvc("rw_ln_w", hc), ALU.mult, cvc("rw_ln_b", hc), ALU.add)
        tt(yT, yT, vT, ALU.add)
        yield
        act(sg, us[:, 7, :], AF.Sigmoid)
        pg = bk()
        mm(pg, smw[:, 256 + hc * 128:256 + (hc + 1) * 128], sg)
        tt(obT[:, hc, :], yT, pg, ALU.mult)


def _prep(inputs, nseq_per_core, ncores):
    x = np.asarray(inputs["x"], np.float32)
    mem = np.asarray(inputs["mem"], np.float32)
    pos = np.asarray(inputs["positions"], np.int32)
    wsl, idx = make_wslots(inputs)
    cst, cst2 = make_cst()
    cvec = make_cvec({k: np.asarray(v) for k, v in inputs.items()})
    smw = make_smallw(inputs)
    maps = []
    for c in range(ncores):
        sl = slice(c * nseq_per_core, (c + 1) * nseq_per_core)
        maps.append({"x": np.ascontiguousarray(x[sl]), "mem": np.ascontiguousarray(mem[sl]),
                     "pos": np.ascontiguousarray(pos[sl]), "cst": cst, "cst2": cst2, "cvec": cvec,
                     "smallw": smw, "wslots": wsl})
    return maps, idx, wsl.shape[0]


def kernel(**inputs):
    B, S, _ = inputs["x"].shape
    ncores = 8
    nseq = B // ncores
    maps, idx, nslots = _prep(inputs, nseq, ncores)
    nc = build(nseq, S, idx, nslots)
    res = run_bass_kernel_spmd(nc, maps, core_ids=list(range(ncores)))
    out = np.concatenate([np.asarray(r["out"]) for r in res.results], axis=0)
    return out.astype(np.float32)
```
